# Optimizing a Trainium2 kernel written in Bass

```python
import jax, jax.numpy as jnp
from jax import lax
import numpy as np

D_MODEL = 2048
BATCH = 8
SEQ = 4096
DEPTH = 4

D_RWKV = D_MODEL // 2
D_POOL = D_MODEL - D_RWKV
HEAD_SIZE = 64
N_HEADS = D_RWKV // HEAD_SIZE
DECAY_LORA = 64
ICLR_LORA = 64
VRES_LORA = 32
POOL_WINDOWS = (2, 4, 8, 16)
N_POOL_GROUPS = len(POOL_WINDOWS)
POOL_GROUP_DIM = D_POOL // N_POOL_GROUPS
D_PLE = 256
N_SHIFT = 3 * D_RWKV + DECAY_LORA + ICLR_LORA
N_IN = N_SHIFT + D_RWKV + 2 * D_POOL
RMS_EPS = 1e-6
GN_EPS = 64e-5
KK_EPS = 1e-12

kernel_name = "hybrid_rwkv7_multiscale_pool_trunk"


def rms_norm(x, g):
    xf = x.astype(jnp.float32)
    y = xf * lax.rsqrt(jnp.mean(xf * xf, axis=-1, keepdims=True) + RMS_EPS)
    return (y * g.astype(jnp.float32)).astype(x.dtype)


def token_shift(s):
    return jnp.pad(s, ((0, 0), (1, 0), (0, 0)))[:, :-1]


def wkv7_scan(r, w, k, v, kk, a):
    B, T, H, N = r.shape

    def step(S, inp):
        r_t, w_t, k_t, v_t, kk_t, a_t = inp
        sa = jnp.einsum('bhvk,bhk->bhv', S, -kk_t)
        S = (S * w_t[:, :, None, :]
             + sa[..., None] * (kk_t * a_t)[:, :, None, :]
             + v_t[..., None] * k_t[:, :, None, :])
        y_t = jnp.einsum('bhvk,bhk->bhv', S, r_t)
        return S, y_t

    xs = tuple(jnp.moveaxis(t, 1, 0) for t in (r, w, k, v, kk, a))
    S0 = jnp.zeros((B, H, N, N), jnp.float32)
    _, ys = lax.scan(step, S0, xs)
    return jnp.moveaxis(ys, 0, 1)


def head_group_norm(y, w, b):
    B, T, H, N = y.shape
    mean = jnp.mean(y, axis=-1, keepdims=True)
    var = jnp.mean(jnp.square(y - mean), axis=-1, keepdims=True)
    yn = (y - mean) * lax.rsqrt(var + GN_EPS)
    return yn.reshape(B, T, H * N) * w.astype(jnp.float32) + b.astype(jnp.float32)


def causal_multiscale_pool(u):
    B, T, _ = u.shape
    ug = u.astype(jnp.float32).reshape(B, T, N_POOL_GROUPS, POOL_GROUP_DIM)
    c = jnp.cumsum(ug, axis=1)
    pos = jnp.arange(1, T + 1, dtype=jnp.float32)
    means = []
    for g, win in enumerate(POOL_WINDOWS):
        cg = c[:, :, g]
        prev = jnp.pad(cg, ((0, 0), (win, 0), (0, 0)))[:, :T]
        cnt = jnp.minimum(pos, jnp.float32(win))
        means.append((cg - prev) / cnt[None, :, None])
    mean = jnp.stack(means, axis=2)
    return mean - ug


def setup_inputs(seed: int = 0) -> dict:
    key = jax.random.key(seed)
    ks = jax.random.split(key, 24)
    f32 = jnp.float32
    nrm = lambda k, s, sc: jax.random.normal(k, s, f32) * sc
    L = DEPTH
    return {
        "x": nrm(ks[0], (BATCH, SEQ, D_MODEL), 1.0),
        "p": nrm(ks[1], (DEPTH, BATCH, SEQ, D_PLE), 1.0),
        "norm_g": 1.0 + nrm(ks[2], (L, D_MODEL), 0.02),
        "w_in": nrm(ks[3], (L, D_MODEL, N_IN), D_MODEL ** -0.5),
        "mu": jax.random.uniform(ks[4], (L, N_SHIFT), f32),
        "w0": jax.random.uniform(ks[5], (L, D_RWKV), f32, minval=-6.0, maxval=1.0),
        "w_up": nrm(ks[6], (L, DECAY_LORA, D_RWKV), 0.1 * DECAY_LORA ** -0.5),
        "a0": nrm(ks[7], (L, D_RWKV), 0.1),
        "a_up": nrm(ks[8], (L, ICLR_LORA, D_RWKV), 0.1 * ICLR_LORA ** -0.5),
        "v0": nrm(ks[9], (L - 1, D_RWKV), 0.1),
        "v_down": nrm(ks[10], (L - 1, D_RWKV, VRES_LORA), D_RWKV ** -0.5),
        "v_up": nrm(ks[11], (L - 1, VRES_LORA, D_RWKV), 0.5 * VRES_LORA ** -0.5),
        "k_k": 0.85 + nrm(ks[12], (L, D_RWKV), 0.02),
        "k_a": 1.0 + nrm(ks[13], (L, D_RWKV), 0.02),
        "r_k": nrm(ks[14], (L, N_HEADS, HEAD_SIZE), 0.1),
        "ln_w": 1.0 + nrm(ks[15], (L, D_RWKV), 0.02),
        "ln_b": nrm(ks[16], (L, D_RWKV), 0.01),
        "w_pool": nrm(ks[17], (L, N_POOL_GROUPS, POOL_GROUP_DIM, POOL_GROUP_DIM), POOL_GROUP_DIM ** -0.5),
        "pool_scale": 0.5 + nrm(ks[18], (L, D_POOL), 0.05),
        "w_out": nrm(ks[19], (L, D_MODEL, D_MODEL), D_MODEL ** -0.5),
        "w_ple": nrm(ks[20], (L, D_PLE, D_MODEL), D_PLE ** -0.5),
        "w_pg": nrm(ks[21], (L, D_MODEL, D_MODEL), D_MODEL ** -0.5),
        "b_pg": nrm(ks[22], (L, D_MODEL), 0.01),
        "final_g": 1.0 + nrm(ks[23], (D_MODEL,), 0.02),
    }


def reference(x, p, norm_g, w_in, mu, w0, w_up, a0, a_up, v0, v_down, v_up, k_k, k_a, r_k,
              ln_w, ln_b, w_pool, pool_scale, w_out, w_ple, w_pg, b_pg, final_g):
    B, T, _ = x.shape
    dt = x.dtype
    f32 = jnp.float32
    v_first = None
    o1 = D_RWKV
    o2 = 2 * D_RWKV
    o3 = 3 * D_RWKV
    o4 = o3 + DECAY_LORA
    for i in range(DEPTH):
        h = rms_norm(x, norm_g[i])
        z = jnp.einsum('btd,dn->btn', h, w_in[i])
        zs = z[..., :N_SHIFT]
        g_rwkv = z[..., N_SHIFT:N_SHIFT + D_RWKV]
        u = z[..., N_SHIFT + D_RWKV:N_SHIFT + D_RWKV + D_POOL]
        g_pool = z[..., N_SHIFT + D_RWKV + D_POOL:]

        zs = zs + (token_shift(zs) - zs) * mu[i]
        r = zs[..., :o1]
        k = zs[..., o1:o2]
        v = zs[..., o2:o3]
        wd = zs[..., o3:o4]
        ad = zs[..., o4:]
        w_loglog = -jax.nn.softplus(-(w0[i] + jnp.tanh(wd) @ w_up[i])) - 0.5
        decay = jnp.exp(-jnp.exp(w_loglog.astype(f32)))
        a = jax.nn.sigmoid(a0[i] + ad @ a_up[i])
        if i == 0:
            v_first = v
        else:
            nu = jax.nn.sigmoid(v0[i - 1] + (v @ v_down[i - 1]) @ v_up[i - 1])
            v = v + (v_first - v) * nu
        kk = (k * k_k[i]).astype(f32).reshape(B, T, N_HEADS, HEAD_SIZE)
        kk = kk / jnp.maximum(jnp.linalg.norm(kk, axis=-1, keepdims=True), KK_EPS)
        k = k * (1.0 + (a - 1.0) * k_a[i])
        rh = r.astype(f32).reshape(B, T, N_HEADS, HEAD_SIZE)
        kh = k.astype(f32).reshape(B, T, N_HEADS, HEAD_SIZE)
        vh = v.astype(f32).reshape(B, T, N_HEADS, HEAD_SIZE)
        ah = a.astype(f32).reshape(B, T, N_HEADS, HEAD_SIZE)
        wh = decay.reshape(B, T, N_HEADS, HEAD_SIZE)
        y = wkv7_scan(rh, wh, kh, vh, kk, ah)
        y = head_group_norm(y, ln_w[i], ln_b[i])
        bonus = jnp.sum(rh * kh * r_k[i].astype(f32), axis=-1, keepdims=True) * vh
        y_rwkv = (y + bonus.reshape(B, T, D_RWKV)).astype(dt) * jax.nn.silu(g_rwkv)

        d = causal_multiscale_pool(u).astype(dt)
        y_pool = jnp.einsum('btgc,gcd->btgd', d, w_pool[i]).reshape(B, T, D_POOL)
        y_pool = y_pool * pool_scale[i] * jax.nn.silu(g_pool)

        y_mix = jnp.concatenate([y_rwkv, y_pool], axis=-1)
        x = x + jnp.einsum('btc,cd->btd', y_mix, w_out[i])

        gate = jax.nn.sigmoid(jnp.einsum('btd,de->bte', x, w_pg[i]) + b_pg[i])
        x = x + gate * jnp.einsum('btq,qd->btd', p[i], w_ple[i])

    return rms_norm(x, final_g)
```

```python
import threading
import numpy as np
import concourse.bass as bass
import concourse.mybir as mybir
from concourse.bass_utils import run_bass_kernel_spmd
from concourse.alu_op_type import AluOpType as ALU

F32 = mybir.dt.float32
BF16 = mybir.dt.bfloat16
I32 = mybir.dt.int32
AF = mybir.ActivationFunctionType

D = 2048
NIN = 6272
TT = 256
C = 64
NCH = TT // C
RING = 4
CHW = 2048
FW = 81 * 2048 + 4 * 1024
SMW = 4352
RMS_EPS = 1e-6
GN_EPS = 64e-5
EH = float(np.exp(-0.5))

def vec_layout(Ld):
    off = {}
    n = 0
    for nm, cnt in [("norm_g", Ld * 16), ("mu", Ld * 25), ("w0", Ld * 8), ("a0", Ld * 8), ("k_k", Ld * 8),
                    ("k_a", Ld * 8), ("ln_w", Ld * 8), ("ln_b", Ld * 8), ("r_k", Ld * 8), ("pool_scale", Ld * 8),
                    ("v0", max(Ld - 1, 1) * 8), ("b_pg", Ld * 16), ("final_g", 16)]:
        off[nm] = n
        n += cnt
    return off, n


class KB:
    def __init__(self, nc):
        self.nc = nc
        self.E = {"pe": nc.tensor, "dve": nc.vector, "act": nc.scalar, "pool": nc.gpsimd, "sp": nc.sync}
        self.semh = {}
        for k in self.E:
            self.semh[k] = nc.semaphore("s_" + k).__enter__()
        self.cnt = {k: 0 for k in self.E}
        self.clock = {k: {} for k in self.E}
        self.res = {}
        self.dcnt = {}
        self.nwait = 0
        self.enabled = True
        self.noself = NOSELF
        self._thr_active = False
        self._tl = threading.local()
        self.efree = {}
        self.cur_tab = None
        self.evt_fin = {}
        self.cnt_model = {}
        self.dry_commit = True

    def _deps(self, r, w):
        d = {}
        for n in r:
            st = self.res.get(n)
            if st and st[0] is not None:
                k, v = st[0]
                if d.get(k, 0) < v:
                    d[k] = v
        for n in w:
            st = self.res.get(n)
            if st:
                if st[0] is not None:
                    k, v = st[0]
                    if d.get(k, 0) < v:
                        d[k] = v
                for k, v in st[1].items():
                    if d.get(k, 0) < v:
                        d[k] = v
        return d

    def _wait(self, eng, d):
        ck = self.clock[eng]
        for k, v in d.items():
            if k == eng and eng in self.noself:
                continue
            if ck.get(k, 0) < v:
                self.E[eng].wait_ge(self.semh[k], v)
                ck[k] = v
                self.nwait += 1

    def _commit(self, ev, r, w):
        k, v = ev
        for n in r:
            st = self.res.setdefault(n, [None, {}])
            if st[1].get(k, 0) < v:
                st[1][k] = v
        for n in w:
            self.res[n] = [ev, {}]

    def run_threads(self, fns):
        fns = [f for f in fns if f is not None]
        n = len(fns)
        if n == 0:
            return
        if n == 1:
            fns[0]()
            return
        assert not self._thr_active
        self._thr_active = True
        self._cond = threading.Condition()
        self._turn = 0
        self._alive = [True] * n
        self._pend = [("start",)] * n
        self._last = 0
        self._exc = None

        def worker(i):
            self._tl.i = i
            with self._cond:
                while self._turn != i:
                    self._cond.wait()
                self._pend[i] = None
            try:
                fns[i]()
            except BaseException as e:
                self._exc = e
            finally:
                with self._cond:
                    self._alive[i] = False
                    self._pend[i] = None
                    self._pick()
                    self._cond.notify_all()
        ths = [threading.Thread(target=worker, args=(i,)) for i in range(n)]
        for t in ths:
            t.start()
        for t in ths:
            t.join()
        self._thr_active = False
        self._tl.i = None
        if self._exc is not None:
            raise self._exc

    def _est_start(self, eng, r, w):
        rdy = 0.0
        for k, v in self._deps(r, w).items():
            f = self.evt_fin.get((k, v), 0.0) + 150.0
            if f > rdy:
                rdy = f
        return max(self.efree.get(eng, 0.0), rdy)

    def _pick(self):
        n = len(self._alive)
        best, bk = None, None
        for d in range(1, n + 1):
            k = (self._last + d) % n
            if not self._alive[k]:
                continue
            p = self._pend[k]
            if p is None:
                continue
            if p[0] == "start":
                st = -2.0
            elif p[0] == "wait":
                if not p[1]():
                    continue
                st = -1.0
            else:
                st = self._est_start(p[1], p[2], p[3])
                if len(p) > 4 and p[4] is not None and p[4] != self.cur_tab:
                    st += 1300.0
            if best is None or st < best:
                best, bk = st, k
        if bk is None:
            if any(self._alive):
                self._exc = RuntimeError("scheduler deadlock: all threads blocked in wait_until")
                for k in range(n):
                    if self._alive[k]:
                        bk = k
                        break
            else:
                self._turn = -1
                return
        self._turn = bk
        self._last = bk

    def _sched(self, desc):
        if not self._thr_active:
            return
        i = getattr(self._tl, "i", None)
        if i is None:
            return
        with self._cond:
            self._pend[i] = desc
            self._pick()
            self._cond.notify_all()
            while self._turn != i:
                self._cond.wait()
            self._pend[i] = None
        if self._exc is not None:
            raise self._exc

    def wait_until(self, pred):
        if pred():
            return
        assert self._thr_active, "wait_until outside threads would block forever"
        self._sched(("wait", pred))
        assert pred()

    def tid(self):
        if not self._thr_active:
            return 0
        i = getattr(self._tl, "i", None)
        return 0 if i is None else i

    DEFDUR = {"dve": 500.0, "act": 420.0, "pool": 650.0, "pe": 35.0, "sp": 100.0}

    def _model(self, eng, r, w, dur, ev, async_dur=None, tab=None):
        st = self._est_start(eng, r, w)
        if tab is not None and tab != self.cur_tab:
            st += 1300.0
            self.cur_tab = tab
        if async_dur is None:
            fin = st + dur
            self.efree[eng] = fin
        else:
            self.efree[eng] = st + dur
            fin = st + async_dur
        self.evt_fin[ev] = fin

    def op(self, eng, fn, r=(), w=(), dur=None, tab=None):
        dur = self.DEFDUR[eng] if dur is None else dur
        self._sched(("op", eng, r, w, tab))
        self.cnt_model[eng] = self.cnt_model.get(eng, 0) + 1
        if not self.enabled:
            self._model(eng, r, w, dur, (eng, self.cnt_model[eng]), tab=tab)
            self._commit((eng, self.cnt_model[eng]), r, w)
            return
        assert self.cnt[eng] + 1 == self.cnt_model[eng]
        self._model(eng, r, w, dur, (eng, self.cnt[eng] + 1), tab=tab)
        self._wait(eng, self._deps(r, w))
        inst = fn(self.E[eng])
        self.cnt[eng] += 1
        inst.then_inc(self.semh[eng], 1)
        self._commit((eng, self.cnt[eng]), r, w)

    def group(self, eng, fns, r=(), w=(), dur=None):
        dur = self.DEFDUR[eng] * len(fns) if dur is None else dur
        self._sched(("op", eng, r, w))
        self.cnt_model[eng] = self.cnt_model.get(eng, 0) + 1
        if not self.enabled:
            self._model(eng, r, w, dur, (eng, self.cnt_model[eng]))
            self._commit((eng, self.cnt_model[eng]), r, w)
            return
        assert self.cnt[eng] + 1 == self.cnt_model[eng]
        self._model(eng, r, w, dur, (eng, self.cnt[eng] + 1))
        self._wait(eng, self._deps(r, w))
        inst = None
        for fn in fns:
            inst = fn(self.E[eng])
        self.cnt[eng] += 1
        inst.then_inc(self.semh[eng], 1)
        self._commit((eng, self.cnt[eng]), r, w)

    def dma(self, q, slot, fns, r=(), w=(), sw=True):
        if sw:
            self._sched(("op", q, r, w))
        self.cnt_model[slot] = self.cnt_model.get(slot, 0) + 16 * len(fns)
        if not self.enabled:
            self._model(q, r, w, 100.0, (slot, self.cnt_model[slot]), async_dur=3000.0)
            self._commit((slot, self.cnt_model[slot]), r, w)
            return
        if slot not in self.semh:
            self.semh[slot] = self.nc.semaphore("d_" + slot).__enter__()
            self.dcnt[slot] = 0
        self._model(q, r, w, 100.0, (slot, self.dcnt[slot] + 16 * len(fns)), async_dur=3000.0)
        self._wait(q, self._deps(r, w))
        for fn in fns:
            fn(self.E[q]).then_inc(self.semh[slot], 16)
            self.dcnt[slot] += 16
        self._commit((slot, self.dcnt[slot]), r, w)

    def final_wait(self, eng, names):
        d = {}
        for n in names:
            st = self.res.get(n)
            if st:
                if st[0] is not None:
                    k, v = st[0]
                    d[k] = max(d.get(k, 0), v)
                for k, v in st[1].items():
                    d[k] = max(d.get(k, 0), v)
        self._wait(eng, d)


def wchunk(kind, i):
    if kind == "in":
        return i * 2048, 2048
    if kind == "out":
        return (49 + i) * 2048, 2048
    if kind == "pg":
        return (65 + i) * 2048, 2048
    if kind == "ple":
        return 81 * 2048 + i * 512, 512
    raise ValueError(kind)


def build(T, Ld, chain_f32=False, stop=99, worder=None):
    dry = worder is None
    NT = T // TT
    nc = bass.Bass("TRN2", target_bir_lowering=False)
    voff, NV = vec_layout(Ld)
    CHD = F32 if chain_f32 else BF16

    xT_d = nc.dram_tensor("xT", [D, T], F32, kind="ExternalInput").ap()
    pT_d = nc.dram_tensor("pT", [Ld, 256, T], F32, kind="ExternalInput").ap()
    wst_d = nc.dram_tensor("wst", [Ld, 128, FW], F32, kind="ExternalInput").ap()
    smw_d = nc.dram_tensor("smw", [Ld, 128, SMW], F32, kind="ExternalInput").ap()
    vecs_d = nc.dram_tensor("vecs", [128, NV], F32, kind="ExternalInput").ap()
    out_d = nc.dram_tensor("outT", [D, T], F32, kind="ExternalOutput").ap()
    wsb_d = nc.dram_tensor("wsb", [Ld, 128, FW], BF16, kind="Internal").ap()
    smb_d = nc.dram_tensor("smb", [Ld, 128, SMW], BF16, kind="Internal").ap()

    kb = KB(nc)
    if dry:
        kb.enabled = False

    def sb(name, shape, dt=F32):
        return nc.sbuf_tensor(name, shape, dt).__enter__()

    def ps(name, shape, dt=F32):
        return nc.psum_tensor(name, shape, dt).__enter__()

    xT = sb("xT_s", [128, 16, TT])
    actbf = sb("actbf", [128, 16, TT], BF16)
    yr = sb("yr", [128, 8, TT], BF16)
    ypool = sb("ypool", [128, 8, TT], BF16)
    ring = [sb("ring%d" % i, [128, CHW], BF16) for i in range(RING)]
    smw = sb("smw_s", [128, SMW], BF16)
    vecs = sb("vecs_s", [128, NV])
    nw0 = sb("nw0", [128, Ld * 8])
    cst = sb("cst", [128, 8])
    zraws = [sb("zraw%d" % i, [128, TT + 1]) for i in range(4)]
    vf = sb("vf", [128, 8, TT])
    vb = sb("vb", [128, 8, TT])
    vbf = sb("vbf", [128, 8, TT], BF16)
    AR = sb("AR", [128, 2, 8, TT], BF16)
    KT = sb("KT", [128, 8, TT], BF16)
    BT = sb("BT", [128, 8, TT], BF16)
    Vt = sb("Vt", [128, 8, NCH, 64], BF16)
    Kt = sb("Kt", [128, 8, NCH, 64], BF16)
    Bt = sb("Bt", [128, 8, NCH, 64], BF16)
    sg = sb("sg", [128, 8, TT], BF16)
    ybuf = sb("ybuf", [128, 8, TT])
    ST = sb("ST", [128, Ld, 8, 64])
    STb = sb("STb", [128, Ld, 8, 64], BF16)
    carry = sb("carry", [128, Ld * 25])
    halo = sb("halo", [128, Ld, 8, 16])
    gC = sb("gC", [128, 8, NCH])
    loraIn = sb("loraIn", [128, TT], BF16)
    pbfs = [sb("pbf%d" % i, [128, 2, TT], BF16) for i in range(2)]
    t32 = sb("t32", [32, TT], BF16)
    ident = sb("ident", [128, 128], BF16)
    onesf = sb("onesf", [128, 128])
    bones = sb("bones", [128, 128])
    maskAR = sb("maskAR", [128, 2, 64])
    maskP = sb("maskP", [128, 64])
    identC = sb("identC", [128, 64])
    rmask = sb("rmask", [128, TT])
    invcnt = sb("invcnt", [128, 4, 16])
    iot = sb("iot", [128, 16], I32)
    TN = ["lw", "L", "eL", "eLm", "eLp", "icl", "r", "k", "kk", "sq", "rn", "kkn", "t", "kp", "d", "g1", "g2"]
    tps = []
    for si in range(2):
        dct = {n: sb("t%d_%s" % (si, n), [128, TT]) for n in TN}
        dct["tb"] = dct["lw"]
        dct["rk"] = dct["L"]
        tps.append(dct)
    ALIAS = {"tb": "lw", "rk": "L"}

    def tr(si, n):
        return ("t", si, ALIAS.get(n, n))
    tzs = sb("t_zs", [128, TT])
    trstd = sb("t_rstd", [128, TT])
    pg1 = sb("t_pg1", [128, TT])
    ptt = sb("t_pt", [128, 16])
    ub = [sb("ub%d" % i, [128, 16 + TT]) for i in range(2)]
    us = [sb("us%d" % i, [128, 16 + TT]) for i in range(2)]
    dpool = [sb("dpool%d" % i, [128, TT], BF16) for i in range(2)]
    ABRBs = [sb("ABRB%d" % i, [128, 8, 2, 64], BF16) for i in range(2)]
    AKRKs = [sb("AKRK%d" % i, [128, 8, 2, 64], BF16) for i in range(2)]
    Pb = [sb("Pb%d" % i, [128, 8, 64], CHD) for i in range(2)]
    Qb = [sb("Qb%d" % i, [128, 8, 64], CHD) for i in range(2)]
    Tb = [sb("Tb%d" % i, [128, 8, 64], CHD) for i in range(2)]
    TFs = [sb("TF%d" % i, [128, 8, 64], CHD) for i in range(2)]
    Wsb = sb("Wsb", [128, 8, 64], CHD)
    Usb = sb("Usb", [128, 8, 64], BF16)

    accp = [ps("accp%d" % i, [128, 512]) for i in range(2)]
    miscp = [ps("miscp%d" % i, [128, 512]) for i in range(2)]
    pA = ps("pA", [128, 1024])
    pX = [ps("pX%d" % i, [128, 512]) for i in range(2)]
    accbanks = [(accp[0], "accp0"), (accp[1], "accp1"), (pX[0], "pX0"), (pX[1], "pX1")]
    acc_i = [0]
    misc_i = [0]

    psmap = {"acc": None, "misc": None}

    def next_acc():
        if kb._thr_active:
            bank = psmap["acc"][kb.tid()]
            return bank[0][:, 0:TT], bank[1]
        i = acc_i[0] % 4
        acc_i[0] += 1
        return accbanks[i][0][:, 0:TT], accbanks[i][1]

    def next_misc():
        if kb._thr_active:
            bank = psmap["misc"][kb.tid()]
            return bank[0][:, 0:TT], bank[1]
        i = misc_i[0] % 2
        misc_i[0] += 1
        return miscp[i][:, 0:TT], "miscp%d" % i

    def V(name, idx):
        o = voff[name] + idx
        return vecs[:, o:o + 1]

    wst2 = wst_d.rearrange("l p (a c) -> (l p a) c", c=2048)
    wsb2 = wsb_d.rearrange("l p (a c) -> (l p a) c", c=2048)
    rows = Ld * 128 * (FW // 2048)
    step = 664
    fns = []
    for r0 in range(0, rows, step):
        r1 = min(rows, r0 + step)
        fns.append(lambda e, r0=r0, r1=r1: e.dma_start(out=wsb2[r0:r1, :], in_=wst2[r0:r1, :]))
    smw2 = smw_d.rearrange("l p (a c) -> (l p a) c", c=256)
    smb2 = smb_d.rearrange("l p (a c) -> (l p a) c", c=256)
    rows2 = Ld * 128 * (SMW // 256)
    for r0 in range(0, rows2, 1088):
        r1 = min(rows2, r0 + 1088)
        fns.append(lambda e, r0=r0, r1=r1: e.dma_start(out=smb2[r0:r1, :], in_=smw2[r0:r1, :]))
    kb.dma("pool", "prolog", fns, r=(), w=("wsb", "smb"))
    kb.dma("sp", "vecs", [lambda e: e.dma_start(out=vecs[:], in_=vecs_d[:, :])], w=("vecs",))

    T0 = tps[0]
    kb.op("dve", lambda e: e.memset(onesf[:], 1.0), w=("onesf",))
    kb.op("dve", lambda e: e.memset(bones[:], 0.0), w=("bones",))
    kb.op("dve", lambda e: e.memset(bones[0:64, 0:64], 1.0), w=("bones",))
    kb.op("dve", lambda e: e.memset(bones[64:128, 64:128], 1.0), w=("bones",))
    kb.op("dve", lambda e: e.memset(rmask[:], 1.0), w=("rmask",))
    kb.op("dve", lambda e: e.memset(rmask[:].rearrange("p (c s) -> p c s", s=64)[:, :, 0:1], 0.0), w=("rmask",))
    for i_ in range(2):
        kb.op("dve", lambda e: e.memset(us[i_][:], 0.0), w=(("us", i_),))
    kb.op("dve", lambda e: e.memset(carry[:], 0.0), w=tuple(("carry", i) for i in range(Ld * 25)))
    kb.op("dve", lambda e: e.memset(halo[:], 0.0), w=tuple(("halo", l_, q_) for l_ in range(Ld) for q_ in range(8)))
    kb.op("dve", lambda e: e.memset(ST[:], 0.0), w=tuple(("ST", l_) for l_ in range(Ld)))
    kb.op("dve", lambda e: e.memset(STb[:], 0.0), w=tuple(("STb", l_) for l_ in range(Ld)))
    for i, val in enumerate([0.0, 1.0, -0.5, RMS_EPS, GN_EPS, -1.0, 1e-18]):
        kb.op("dve", lambda e, i=i, val=val: e.memset(cst[:, i:i + 1], val), w=("cst",))
    C0, C1, CM05, CRMS, CGN = (cst[:, i:i + 1] for i in range(5))
    CTINY = cst[:, 6:7]
    kb.op("pool", lambda e: e.affine_select(out=T0["t"][:, 0:128], in_=onesf[:], pattern=[[1, 128]],
                                            compare_op=ALU.is_equal, fill=0.0, base=0, channel_multiplier=-1),
          r=("onesf",), w=(tr(0, "t"),))
    kb.op("dve", lambda e: e.tensor_copy(out=ident[:], in_=T0["t"][:, 0:128]), r=(tr(0, "t"),), w=("ident",))
    for hp in range(2):
        sl = slice(hp * 64, hp * 64 + 64)
        kb.op("pool", lambda e: e.affine_select(out=identC[sl, :], in_=onesf[sl, 0:64], pattern=[[1, 64]],
                                                compare_op=ALU.is_equal, fill=0.0, base=0, channel_multiplier=-1),
              r=("onesf",), w=("identC",))
        kb.op("pool", lambda e: e.affine_select(out=maskAR[sl, 0, :], in_=onesf[sl, 0:64], pattern=[[1, 64]],
                                                compare_op=ALU.is_gt, fill=0.0, base=0, channel_multiplier=-1),
              r=("onesf",), w=("maskAR",))
        kb.op("pool", lambda e: e.affine_select(out=maskAR[sl, 1, :], in_=onesf[sl, 0:64], pattern=[[1, 64]],
                                                compare_op=ALU.is_ge, fill=0.0, base=0, channel_multiplier=-1),
              r=("onesf",), w=("maskAR",))
        kb.op("pool", lambda e: e.affine_select(out=maskP[sl, :], in_=onesf[sl, 0:64], pattern=[[-1, 64]],
                                                compare_op=ALU.is_gt, fill=0.0, base=0, channel_multiplier=1),
              r=("onesf",), w=("maskP",))
    kb.op("pool", lambda e: e.iota(iot[:], pattern=[[1, 16]], base=1, channel_multiplier=0), w=("iot",))
    kb.op("dve", lambda e: e.tensor_copy(out=T0["t"][:, 0:16], in_=iot[:]), r=("iot",), w=(tr(0, "t"),))
    for g in range(4):
        kb.op("dve", lambda e, g=g: e.tensor_scalar(out=T0["d"][:, 0:16], in0=T0["t"][:, 0:16],
                                                    scalar1=float(2 ** (g + 1)), scalar2=None, op0=ALU.min),
              r=(tr(0, "t"),), w=(tr(0, "d"),))
        kb.op("dve", lambda e, g=g: e.reciprocal(out=invcnt[:, g, :], in_=T0["d"][:, 0:16]), r=(tr(0, "d"),),
              w=("invcnt",))
    kb.op("dve", lambda e: e.tensor_scalar(out=nw0[:], in0=vecs[:, voff["w0"]:voff["w0"] + Ld * 8], scalar1=-1.0,
                                           scalar2=None, op0=ALU.mult), r=("vecs",), w=("nw0",))

    order_rec = []
    stream = worder if worder is not None else []
    wpos = [0]
    wissued = [0]
    released = set()

    def issue_w():
        n = wissued[0]
        l, kind, i = stream[n]
        o, sz = wchunk(kind, i)
        slot = n % RING
        kb.dma("sp", "ring%d" % slot,
               [lambda e: e.dma_start(out=ring[slot][:, 0:sz], in_=wsb_d[l, :, o:o + sz])],
               r=("wsb",), w=(("ring", slot),), sw=False)
        wissued[0] += 1

    def pump():
        if dry:
            return
        while wissued[0] < len(stream) and (wissued[0] < RING or (wissued[0] - RING) in released):
            issue_w()

    def next_w(l, kind, i):
        n = wpos[0]
        wpos[0] += 1
        if dry:
            order_rec.append((l, kind, i))
        else:
            assert stream[n] == (l, kind, i), ("weight order mismatch", n, stream[n], (l, kind, i))
            pump()
            assert wissued[0] > n, "weight ring deadlock"
        slot = n % RING
        return ring[slot], ("ring", slot), n

    def release_w(n):
        released.add(n)
        pump()

    def mm_proj(l, kind, i, rhs_of, rhs_res, nk=16):
        wt, wres, wn = next_w(l, kind, i)
        acc, ares = next_acc()
        w3 = wt[:].rearrange("p (k c) -> p k c", c=128)
        kb.group("pe", [lambda e, kc=kc: e.matmul(acc, w3[:, kc, :], rhs_of(kc), start=(kc == 0), stop=(kc == nk - 1))
                        for kc in range(nk)], r=(wres,) + tuple(rhs_res), w=(ares,), dur=137.0 * nk)
        release_w(wn)
        return acc, ares

    def lerp(si, acc, ares, l, cc, out_ap, out_res, zi=None, dn="d"):
        ci = l * 25 + cc
        zi = si if zi is None else zi
        zraw = zraws[zi]
        zr, zr0 = ("zraw", zi), ("zraw0", zi)
        td = tps[si][dn]
        kb.op("act", lambda e: e.activation(out=zraw[:, 1:TT + 1], in_=acc, func=AF.Copy), r=(ares,), w=(zr,))
        kb.op("pool", lambda e: e.tensor_copy(out=zraw[:, 0:1], in_=carry[:, ci:ci + 1]), r=(("carry", ci),),
              w=(zr0,))
        kb.op("pool", lambda e: e.tensor_tensor(out=td[:], in0=zraw[:, 0:TT], in1=zraw[:, 1:TT + 1],
                                                op=ALU.subtract), r=(zr, zr0), w=(tr(si, dn),))
        kb.op("dve", lambda e: e.scalar_tensor_tensor(out=out_ap, in0=td[:], scalar=V("mu", ci),
                                                      in1=zraw[:, 1:TT + 1], op0=ALU.mult, op1=ALU.add),
              r=(tr(si, dn), zr, "vecs"), w=(out_res,))
        kb.op("pool", lambda e: e.tensor_copy(out=carry[:, ci:ci + 1], in_=zraw[:, TT:TT + 1]), r=(zr,),
              w=(("carry", ci),))

    def bsum(src_ap, src_res):
        m, mres = next_misc()
        kb.group("pe", [lambda e: e.matmul(m, bones[:], src_ap, start=True, stop=True)],
                 r=("bones",) + tuple(src_res), w=(mres,), dur=250.0)
        return m, mres

    hsl = [slice(0, 64), slice(64, 128)]
    PS2 = {"acc": [(accp[0], "accp0"), (accp[1], "accp1")], "misc": [(miscp[0], "miscp0"), (miscp[1], "miscp1")]}
    pA_lo, pA_hi = pA[:, 0:512], pA[:, 512:1024]
    PS2C = {"acc": [(accp[0], "accp0"), (accp[1], "accp1"), (pX[0], "pX0"), (pX[1], "pX1")],
            "misc": [(miscp[0], "miscp0"), (miscp[1], "miscp1"), (pA_lo, "pA0"), (pA_hi, "pA1")]}
    PS4 = {"acc": [None] * 4, "misc": [(miscp[0], "miscp0"), (miscp[1], "miscp1"), (accp[0], "accp0"), (accp[1], "accp1")]}
    PS_SCAN = {"acc": [None, None, (accp[1], "accp1")], "misc": [None, None, (miscp[1], "miscp1")]}

    def rms_norm(l_unused, write_fn):
        m, mres = next_misc()
        for c in range(16):
            sqn = ["sq", "kk", "rn"][c % 3]
            kb.op("act", lambda e: e.activation(out=T0[sqn][:], in_=xT[:, c, :], func=AF.Square),
                  r=(("xT", c),), w=(tr(0, sqn),))
            kb.group("pe", [lambda e: e.matmul(m, onesf[:], T0[sqn][:], start=(c == 0), stop=(c == 15))],
                     r=("onesf", tr(0, sqn)), w=(mres,))
        kb.op("act", lambda e: e.activation(out=T0["t"][:], in_=m, func=AF.Ln, bias=CRMS, scale=1.0 / D),
              r=(mres, "cst"), w=(tr(0, "t"),), tab="exp")
        kb.op("act", lambda e: e.activation(out=trstd[:], in_=T0["t"][:], func=AF.Exp, scale=-0.5),
              r=(tr(0, "t"),), w=("t_rstd",), tab="exp")
        for c in range(16):
            write_fn(c)

    for ti in range(NT):
        t0 = ti * TT
        for l in range(Ld):
            if l == 0:
                kb.dma("sp", "xload",
                       [lambda e, c4=c4: e.dma_start(out=xT[:, c4 * 4:c4 * 4 + 4, :],
                                                     in_=xT_d.rearrange("(c p) t -> p c t", p=128)[:, c4 * 4:c4 * 4 + 4, t0:t0 + TT])
                        for c4 in range(4)],
                       w=tuple(("xT", c) for c in range(16)))
            kb.dma("sp", "smw", [lambda e: e.dma_start(out=smw[:], in_=smb_d[l, :, :])], r=("smb",), w=("smw",))
            pbi = (ti * Ld + l) % 2
            pbf = pbfs[pbi]
            pbres = ("pbf", pbi)
            kb.dma("pool", "pload%d" % pbi,
                   [lambda e: e.dma_start(out=pbf[:], in_=pT_d[l].rearrange("(k q) t -> q k t", q=128)[:, :, t0:t0 + TT])],
                   w=(pbres,))
            lora = smw[:, 0:1024]
            vdn = smw[:, 1024:1280].rearrange("p (k c) -> p k c", c=32)
            vup = smw[:, 1280:2304]
            wpool = smw[:, 2304:4352].rearrange("p (g e d) -> p g e d", g=4, e=2)

            rms_norm(l, lambda c: kb.op(
                "dve", lambda e: e.scalar_tensor_tensor(out=actbf[:, c, :], in0=xT[:, c, :],
                                                        scalar=V("norm_g", l * 16 + c), in1=trstd[:],
                                                        op0=ALU.mult, op1=ALU.mult),
                r=(("xT", c), "t_rstd", "vecs"), w=(("actbf", c),)))
            hres = tuple(("actbf", c) for c in range(16))
            hrhs = lambda kc: actbf[:, kc, :]

            acc, ares = mm_proj(l, "in", 24, hrhs, hres)
            lerp(0, acc, ares, l, 24, tzs[:], "t_zs")
            kb.op("act", lambda e: e.activation(out=loraIn[0:64, :], in_=tzs[0:64, :], func=AF.Tanh),
                  r=("t_zs",), w=("loraIn0",), tab="sig")
            kb.op("act", lambda e: e.activation(out=loraIn[64:128, :], in_=tzs[64:128, :], func=AF.Copy),
                  r=("t_zs",), w=("loraIn1",))

            vsrc = vf if l == 0 else vb
            vname = "vf" if l == 0 else "vb"

            def vchain(si):
                for j in range(si, 8, 2):
                    acc, ares = mm_proj(l, "in", 16 + j, hrhs, hres)
                    lerp(si, acc, ares, l, 16 + j, vsrc[:, j, :], (vname, j))
                    kb.op("act", lambda e: e.activation(out=vbf[:, j, :], in_=vsrc[:, j, :], func=AF.Copy),
                          r=((vname, j),), w=(("vbf", j),))
            psmap.update(PS2)
            kb.run_threads([lambda: vchain(0), lambda: vchain(1)])
            if l > 0:
                m, mres = next_misc()
                kb.group("pe", [lambda e, j=j: e.matmul(m[0:32, :], vdn[:, j, :], vbf[:, j, :], start=(j == 0),
                                                        stop=(j == 7)) for j in range(8)],
                         r=("smw",) + tuple(("vbf", j) for j in range(8)), w=(mres,))
                kb.op("act", lambda e: e.activation(out=t32[:], in_=m[0:32, :], func=AF.Copy), r=(mres,), w=("t32",))

                def nuchain(si):
                    tp = tps[si]
                    for j in range(si, 8, 2):
                        m, mres = next_misc()
                        kb.group("pe", [lambda e: e.matmul(m, vup[0:32, j * 128:(j + 1) * 128], t32[:], start=True,
                                                           stop=True)], r=("smw", "t32"), w=(mres,))
                        kb.op("act", lambda e: e.activation(out=tp["g1"][:], in_=m, func=AF.Sigmoid,
                                                            bias=V("v0", (l - 1) * 8 + j), scale=1.0),
                              r=(mres, "vecs"), w=(tr(si, "g1"),), tab="sig")
                        kb.op("pool", lambda e: e.tensor_tensor(out=tp["g2"][:], in0=vf[:, j, :], in1=vb[:, j, :],
                                                               op=ALU.subtract), r=(("vf", j), ("vb", j)),
                              w=(tr(si, "g2"),))
                        kb.op("dve", lambda e: e.tensor_tensor(out=tp["g2"][:], in0=tp["g2"][:], in1=tp["g1"][:],
                                                               op=ALU.mult), r=(tr(si, "g2"), tr(si, "g1")),
                              w=(tr(si, "g2"),))
                        kb.op("pool", lambda e: e.tensor_tensor(out=vb[:, j, :], in0=vb[:, j, :], in1=tp["g2"][:],
                                                               op=ALU.add), r=(("vb", j), tr(si, "g2")),
                              w=(("vb", j),))
                        kb.op("act", lambda e: e.activation(out=vbf[:, j, :], in_=vb[:, j, :], func=AF.Copy),
                              r=(("vb", j),), w=(("vbf", j),))
                psmap.update(PS2)
                kb.run_threads([lambda: nuchain(0), lambda: nuchain(1)])

            pTr = pA[:].bitcast(BF16).rearrange("p (j c k) -> p j c k", j=8, c=NCH)

            def to_tokmajor(src, src_res_fn, dst, dst_res):
                fl = []
                for j in range(8):
                    for c in range(NCH):
                        for hp in range(2):
                            fl.append(lambda e, j=j, c=c, hp=hp: e.transpose(
                                pTr[hsl[hp], j, c, :], src[hsl[hp], j, c * 64:(c + 1) * 64],
                                ident[hsl[hp], hp * 64:hp * 64 + 64]))
                kb.group("pe", fl, r=("ident",) + tuple(src_res_fn(j) for j in range(8)), w=("pA0", "pA1"),
                         dur=4800.0)
                kb.op("act", lambda e: e.activation(out=dst[:], in_=pTr, func=AF.Copy), r=("pA0", "pA1"), w=(dst_res,),
                      dur=2000.0)

            to_tokmajor(vbf, lambda j: ("vbf", j), Vt, "Vt")

            doneX = [False] * 8
            doneY = [False] * 8

            def thrX(si):
                tp = tps[si]
                R_ = lambda n: tr(si, n)
                for j in range(si, 8, 2):
                    cj = l * 8 + j
                    if j >= 2:
                        kb.wait_until(lambda: doneY[j - 2])
                    m, mres = next_misc()
                    kb.group("pe", [lambda e: e.matmul(m, lora[0:64, j * 128:(j + 1) * 128], loraIn[0:64, :],
                                                       start=True, stop=True)], r=("smw", "loraIn0"), w=(mres,))
                    kb.op("act", lambda e: e.activation(out=tp["lw"][:], in_=m, func=AF.Sigmoid, bias=V("w0", cj),
                                                        scale=1.0), r=(mres, "vecs"), w=(R_("lw"),), tab="sig")
                    kb.op("dve", lambda e: e.tensor_tensor_scan(out=tp["L"][:], data0=rmask[:], data1=tp["lw"][:],
                                                                initial=0.0, op0=ALU.mult, op1=ALU.add),
                          r=("rmask", R_("lw")), w=(R_("L"),))
                    kb.op("act", lambda e: e.activation(out=tp["eL"][:], in_=tp["L"][:], func=AF.Exp, scale=-EH),
                          r=(R_("L"),), w=(R_("eL"),), tab="exp")
                    kb.op("act", lambda e: e.activation(out=tp["eLm"][:], in_=tp["L"][:], func=AF.Exp, scale=EH),
                          r=(R_("L"),), w=(R_("eLm"),), tab="exp")
                    kb.op("pool", lambda e: e.tensor_tensor(out=tp["t"][:], in0=tp["L"][:], in1=tp["lw"][:],
                                                            op=ALU.subtract), r=(R_("L"), R_("lw")), w=(R_("t"),))
                    kb.op("act", lambda e: e.activation(out=tp["eLp"][:], in_=tp["t"][:], func=AF.Exp, scale=-EH),
                          r=(R_("t"),), w=(R_("eLp"),), tab="exp")
                    kb.op("pool", lambda e: e.tensor_copy(
                        out=gC[:, j, :], in_=tp["eL"][:].rearrange("p (c s) -> p c s", s=64)[:, :, 63]),
                        r=(R_("eL"),), w=(("gC", j),))
                    m, mres = next_misc()
                    kb.group("pe", [lambda e: e.matmul(m, lora[64:128, j * 128:(j + 1) * 128], loraIn[64:128, :],
                                                       start=True, stop=True)], r=("smw", "loraIn1"), w=(mres,))
                    kb.op("act", lambda e: e.activation(out=tp["icl"][:], in_=m, func=AF.Sigmoid, bias=V("a0", cj),
                                                        scale=1.0), r=(mres, "vecs"), w=(R_("icl"),), tab="sig")
                    acc, ares = mm_proj(l, "in", j, hrhs, hres)
                    lerp(si, acc, ares, l, j, tp["r"][:], R_("r"))
                    kb.op("pool", lambda e: e.tensor_tensor(out=AR[:, 1, j, :], in0=tp["r"][:], in1=tp["eL"][:],
                                                           op=ALU.mult), r=(R_("r"), R_("eL")), w=(("RT", j),))
                    doneX[j] = True

            def thrY(si):
                tp = tps[si]
                R_ = lambda n: tr(si, n)
                for j in range(si, 8, 2):
                    cj = l * 8 + j
                    acc, ares = mm_proj(l, "in", 8 + j, hrhs, hres)
                    lerp(si, acc, ares, l, 8 + j, tp["k"][:], R_("k"), zi=2 + si, dn="sq")
                    kb.op("dve", lambda e: e.tensor_scalar(out=tp["kk"][:], in0=tp["k"][:], scalar1=V("k_k", cj),
                                                           scalar2=None, op0=ALU.mult), r=(R_("k"), "vecs"),
                          w=(R_("kk"),))
                    kb.op("act", lambda e: e.activation(out=tp["sq"][:], in_=tp["kk"][:], func=AF.Square),
                          r=(R_("kk"),), w=(R_("sq"),))
                    m, mres = bsum(tp["sq"][:], (R_("sq"),))
                    kb.op("act", lambda e: e.activation(out=tp["rn"][:], in_=m, func=AF.Ln, bias=CTINY, scale=1.0),
                          r=(mres, "cst"), w=(R_("rn"),), tab="exp")
                    kb.op("act", lambda e: e.activation(out=tp["rn"][:], in_=tp["rn"][:], func=AF.Exp, scale=-0.5),
                          r=(R_("rn"),), w=(R_("rn"),), tab="exp")
                    kb.op("pool", lambda e: e.tensor_tensor(out=tp["kkn"][:], in0=tp["kk"][:], in1=tp["rn"][:],
                                                           op=ALU.mult), r=(R_("kk"), R_("rn")), w=(R_("kkn"),))
                    kb.wait_until(lambda: doneX[j])
                    kb.op("dve", lambda e: e.tensor_scalar(out=tp["g1"][:], in0=tp["icl"][:], scalar1=-1.0,
                                                           scalar2=V("k_a", cj), op0=ALU.add, op1=ALU.mult),
                          r=(R_("icl"), "vecs"), w=(R_("g1"),))
                    kb.op("dve", lambda e: e.scalar_tensor_tensor(out=tp["kp"][:], in0=tp["g1"][:], scalar=1.0,
                                                                  in1=tp["k"][:], op0=ALU.add, op1=ALU.mult),
                          r=(R_("g1"), R_("k")), w=(R_("kp"),))
                    kb.op("dve", lambda e: e.tensor_tensor(out=KT[:, j, :], in0=tp["kp"][:], in1=tp["eLm"][:],
                                                           op=ALU.mult), r=(R_("kp"), R_("eLm")), w=(("KT", j),))
                    kb.op("dve", lambda e: e.scalar_tensor_tensor(out=AR[:, 0, j, :], in0=tp["kkn"][:], scalar=-1.0,
                                                                  in1=tp["eLp"][:], op0=ALU.mult, op1=ALU.mult),
                          r=(R_("kkn"), R_("eLp")), w=(("AT", j),))
                    kb.op("pool", lambda e: e.tensor_tensor(out=tp["g2"][:], in0=tp["kkn"][:], in1=tp["icl"][:],
                                                           op=ALU.mult), r=(R_("kkn"), R_("icl")), w=(R_("g2"),))
                    kb.op("pool", lambda e: e.tensor_tensor(out=BT[:, j, :], in0=tp["g2"][:], in1=tp["eLm"][:],
                                                           op=ALU.mult), r=(R_("g2"), R_("eLm")), w=(("BT", j),))
                    kb.op("dve", lambda e: e.scalar_tensor_tensor(out=tp["kk"][:], in0=tp["r"][:],
                                                                  scalar=V("r_k", cj), in1=tp["kp"][:],
                                                                  op0=ALU.mult, op1=ALU.mult),
                          r=(R_("r"), R_("kp"), "vecs"), w=(R_("kk"),))
                    m, mres = bsum(tp["kk"][:], (R_("kk"),))
                    kb.op("dve", lambda e: e.tensor_tensor(out=vb[:, j, :], in0=m, in1=vsrc[:, j, :], op=ALU.mult),
                          r=(mres, (vname, j)), w=(("vb", j),))

                    doneY[j] = True

            def gates_pool():
                for j in range(8):
                    acc, ares = mm_proj(l, "in", 25 + j, hrhs, hres)
                    kb.op("act", lambda e: e.activation(out=sg[:, j, :], in_=acc, func=AF.Silu), r=(ares,),
                          w=(("sg", j),), tab="silu")
                for g in range(4):
                    win = 2 ** (g + 1)
                    for e_ in range(2):
                        q = 2 * g + e_
                        acc, ares = mm_proj(l, "in", 33 + q, hrhs, hres)
                        u = ub[e_]
                        kb.op("act", lambda e: e.activation(out=u[:, 16:16 + TT], in_=acc, func=AF.Copy), r=(ares,),
                              w=(("ub", e_),))
                        kb.op("pool", lambda e: e.tensor_copy(out=u[:, 0:16], in_=halo[:, l, q, :]),
                              r=(("halo", l, q),), w=(("ubh", e_),))
                        cur, curres = u, (("ub", e_), ("ubh", e_))
                        stp = 1
                        k_ = 0
                        while stp < win:
                            dst = us[k_ % 2]
                            dres = ("us", k_ % 2)
                            kb.op("pool", lambda e: e.tensor_tensor(out=dst[:, stp:16 + TT], in0=cur[:, stp:16 + TT],
                                                                    in1=cur[:, 0:16 + TT - stp], op=ALU.add),
                                  r=curres, w=(dres,))
                            cur, curres = dst, (dres,)
                            stp *= 2
                            k_ += 1
                        kb.op("pool", lambda e: e.tensor_copy(out=halo[:, l, q, :], in_=u[:, TT:TT + 16]),
                              r=(("ub", e_), ("ubh", e_)), w=(("halo", l, q),))
                        kb.op("dve", lambda e: e.scalar_tensor_tensor(out=dpool[e_][:], in0=cur[:, 16:16 + TT],
                                                                      scalar=1.0 / win, in1=u[:, 16:16 + TT],
                                                                      op0=ALU.mult, op1=ALU.subtract),
                              r=curres + (("ub", e_),), w=(("dpool", e_),))
                        if ti == 0:
                            kb.op("dve", lambda e: e.tensor_tensor(out=ptt[:], in0=cur[:, 16:32],
                                                                   in1=invcnt[:, g, :], op=ALU.mult),
                                  r=curres + ("invcnt",), w=("t_pt",))
                            kb.op("dve", lambda e: e.tensor_tensor(out=dpool[e_][:, 0:16], in0=ptt[:],
                                                                   in1=u[:, 16:32], op=ALU.subtract),
                                  r=("t_pt", ("ub", e_)), w=(("dpool", e_),))
                    for od in range(2):
                        q = 2 * g + od
                        m, mres = next_misc()
                        kb.group("pe", [lambda e, e_=e_: e.matmul(m, wpool[:, g, e_, od * 128:(od + 1) * 128],
                                                                  dpool[e_][:], start=(e_ == 0), stop=(e_ == 1))
                                        for e_ in range(2)], r=("smw", ("dpool", 0), ("dpool", 1)), w=(mres,))
                        acc, ares = mm_proj(l, "in", 41 + q, hrhs, hres)
                        kb.op("act", lambda e: e.activation(out=pg1[:], in_=acc, func=AF.Silu), r=(ares,),
                              w=("t_pg1",), tab="silu")
                        kb.op("dve", lambda e: e.scalar_tensor_tensor(out=ypool[:, q, :], in0=m,
                                                                      scalar=V("pool_scale", l * 8 + q),
                                                                      in1=pg1[:], op0=ALU.mult, op1=ALU.mult),
                              r=(mres, "t_pg1", "vecs"), w=(("ypool", q),))

            psmap.update(PS2C)
            kb.run_threads([lambda: thrX(0), lambda: thrX(1), lambda: thrY(0), lambda: thrY(1)])
            to_tokmajor(KT, lambda j: ("KT", j), Kt, "Kt")
            to_tokmajor(BT, lambda j: ("BT", j), Bt, "Bt")

            STl = ST[:, l, :, :]
            STbl = STb[:, l, :, :]
            allAT = tuple(("AT", j) for j in range(8))
            allRT = tuple(("RT", j) for j in range(8))
            allKT = tuple(("KT", j) for j in range(8))
            allBT = tuple(("BT", j) for j in range(8))
            pA4 = pA[:].rearrange("p (j a t) -> p j a t", j=8, a=2)
            pX3 = [pX[i][:].rearrange("p (j t) -> p j t", j=8) for i in range(2)]
            pW3 = accp[0][:].rearrange("p (j t) -> p j t", j=8)
            pU3 = accp[0][:].rearrange("p (j t) -> p j t", j=8)
            pY3 = miscp[0][:].rearrange("p (j t) -> p j t", j=8)
            pS3 = miscp[0][:].rearrange("p (j t) -> p j t", j=8)
            mAR = maskAR[:].unsqueeze(1).broadcast_to([128, 8, 2, 64])
            mP = maskP[:].unsqueeze(1).broadcast_to([128, 8, 64])
            iC = identC[:].unsqueeze(1).broadcast_to([128, 8, 64])
            JH = [(j, hp) for j in range(8) for hp in range(2)]

            def stageA(c):
                tok = slice(c * 64, (c + 1) * 64)
                ABRB, AKRK, TF = ABRBs[c % 2], AKRKs[c % 2], TFs[c % 2]
                rAB, rAK, rTF = ("ABRB", c % 2), ("AKRK", c % 2), ("TF", c % 2)
                kb.group("pe", [lambda e, j=j, hp=hp: e.matmul(pA4[hsl[hp], j, :, :], BT[hsl[hp], j, tok],
                                                               AR[hsl[hp], :, j, tok], start=True, stop=True)
                                for j, hp in JH], r=allBT + allAT + allRT, w=("pA0", "pA1"))
                kb.op("dve", lambda e: e.tensor_tensor(out=ABRB[:], in0=pA4, in1=mAR, op=ALU.mult),
                      r=("pA0", "pA1", "maskAR"), w=(rAB,))
                kb.group("pe", [lambda e, j=j, hp=hp: e.matmul(pX3[0][hsl[hp], j, :], AR[hsl[hp], 0, j, tok],
                                                               BT[hsl[hp], j, tok], start=True, stop=True)
                                for j, hp in JH], r=allBT + allAT, w=("pX0",))
                kb.op("dve", lambda e: e.tensor_tensor(out=Pb[0][:], in0=pX3[0], in1=mP, op=ALU.mult),
                      r=("pX0", "maskP"), w=("Pb0",))
                kb.group("pe", [lambda e, j=j, hp=hp: e.matmul(pA4[hsl[hp], j, :, :], KT[hsl[hp], j, tok],
                                                               AR[hsl[hp], :, j, tok], start=True, stop=True)
                                for j, hp in JH], r=allKT + allAT + allRT, w=("pA0", "pA1"))
                kb.op("act", lambda e: e.activation(out=Qb[0][:], in_=ABRB[:, :, 0, :], func=AF.Copy),
                      r=(rAB,), w=("Qb0",))
                kb.op("dve", lambda e: e.tensor_tensor(out=AKRK[:], in0=pA4, in1=mAR, op=ALU.mult),
                      r=("pA0", "pA1", "maskAR"), w=(rAK,))
                kb.op("dve", lambda e: e.tensor_tensor(out=Tb[0][:], in0=ABRB[:, :, 0, :], in1=iC, op=ALU.add),
                      r=(rAB, "identC"), w=("Tb0",))
                def t_update(lev):
                    a, b = lev % 2, (lev + 1) % 2
                    kb.group("pe", [lambda e, j=j, hp=hp: e.matmul(pA4[hsl[hp], j, 0, :], Pb[b][hsl[hp], j, :],
                                                                   Tb[a][hsl[hp], j, :], start=True, stop=True)
                                    for j, hp in JH], r=("Pb%d" % b, "Tb%d" % a), w=("pA0", "pA1"), dur=540.0)
                    tdst, tdres = (Tb[b], "Tb%d" % b) if lev < 4 else (TF, rTF)
                    kb.op("dve", lambda e: e.tensor_tensor(out=tdst[:], in0=pA4[:, :, 0, :], in1=Tb[a][:],
                                                           op=ALU.add), r=("pA0", "pA1", "Tb%d" % a), w=(tdres,),
                          dur=650.0)

                for lev in range(5):
                    a, b = lev % 2, (lev + 1) % 2
                    kb.group("pe", [lambda e, j=j, hp=hp: e.matmul(pX3[0][hsl[hp], j, :], Qb[a][hsl[hp], j, :],
                                                                   Pb[a][hsl[hp], j, :], start=True, stop=True)
                                    for j, hp in JH], r=("Qb%d" % a, "Pb%d" % a), w=("pX0",), dur=540.0)
                    kb.op("dve", lambda e: e.tensor_copy(out=Pb[b][:], in_=pX3[0]), r=("pX0",), w=("Pb%d" % b,),
                          dur=650.0)
                    if lev < 4:
                        kb.group("pe", [lambda e, j=j, hp=hp: e.matmul(pX3[1][hsl[hp], j, :], Pb[a][hsl[hp], j, :],
                                                                       Qb[a][hsl[hp], j, :], start=True, stop=True)
                                        for j, hp in JH], r=("Qb%d" % a, "Pb%d" % a), w=("pX1",), dur=540.0)
                        kb.op("act", lambda e: e.activation(out=Qb[b][:], in_=pX3[1], func=AF.Copy),
                              r=("pX1",), w=("Qb%d" % b,), dur=650.0)
                    if lev >= 1:
                        t_update(lev - 1)
                t_update(4)

            def stageB(c):
                tok = slice(c * 64, (c + 1) * 64)
                ABRB, AKRK, TF = ABRBs[c % 2], AKRKs[c % 2], TFs[c % 2]
                rAB, rAK, rTF = ("ABRB", c % 2), ("AKRK", c % 2), ("TF", c % 2)
                fl = []
                for j, hp in JH:
                    fl.append(lambda e, j=j, hp=hp: e.matmul(pW3[hsl[hp], j, :], AR[hsl[hp], 0, j, tok],
                                                             STbl[hsl[hp], j, :], start=True, stop=False))
                    fl.append(lambda e, j=j, hp=hp: e.matmul(pW3[hsl[hp], j, :], AKRK[hsl[hp], j, 0, :],
                                                             Vt[hsl[hp], j, c, :], start=False, stop=True))
                kb.group("pe", fl, dur=33.0 * len(fl), r=allAT + (("STb", l), rAK, "Vt"), w=("accp0",))
                kb.op("act", lambda e: e.activation(out=Wsb[:], in_=pW3, func=AF.Copy), r=("accp0",), w=("Wsb",))
                kb.group("pe", [lambda e, j=j, hp=hp: e.matmul(pU3[hsl[hp], j, :], TF[hsl[hp], j, :],
                                                               Wsb[hsl[hp], j, :], start=True, stop=True)
                                for j, hp in JH], r=(rTF, "Wsb"), w=("accp0",))
                kb.op("act", lambda e: e.activation(out=Usb[:], in_=pU3, func=AF.Copy), r=("accp0",), w=("Usb",))
                fl = []
                for j, hp in JH:
                    o = pY3[hsl[hp], j, :]
                    fl.append(lambda e, j=j, hp=hp, o=o: e.matmul(o, STbl[hsl[hp], j, :], AR[hsl[hp], 1, j, tok],
                                                                  start=True, stop=False))
                    fl.append(lambda e, j=j, hp=hp, o=o: e.matmul(o, Usb[hsl[hp], j, :], ABRB[hsl[hp], j, 1, :],
                                                                  start=False, stop=False))
                    fl.append(lambda e, j=j, hp=hp, o=o: e.matmul(o, Vt[hsl[hp], j, c, :], AKRK[hsl[hp], j, 1, :],
                                                                  start=False, stop=True))
                kb.group("pe", fl, dur=33.0 * len(fl), r=allRT + (("STb", l), "Usb", rAB, rAK, "Vt"), w=("miscp0",))
                kb.op("act", lambda e: e.activation(out=ybuf[:, :, tok], in_=pY3, func=AF.Copy), r=("miscp0",),
                      w=(("ybuf", c),))
                fl = []
                for j, hp in JH:
                    o = pS3[hsl[hp], j, :]
                    fl.append(lambda e, j=j, hp=hp, o=o: e.matmul(o, Bt[hsl[hp], j, c, :], Usb[hsl[hp], j, :],
                                                                  start=True, stop=False))
                    fl.append(lambda e, j=j, hp=hp, o=o: e.matmul(o, Kt[hsl[hp], j, c, :], Vt[hsl[hp], j, c, :],
                                                                  start=False, stop=True))
                kb.group("pe", fl, dur=33.0 * len(fl), r=("Bt", "Kt", "Vt", "Usb"), w=("miscp0",))
                kb.op("dve", lambda e: e.tensor_tensor(out=STl, in0=pS3, in1=STl, op=ALU.add),
                      r=("miscp0", ("ST", l)), w=(("ST", l),))
                gcb = gC[:, :, c:c + 1].broadcast_to([128, 8, 64])
                kb.op("dve", lambda e: e.tensor_tensor(out=STl, in0=STl, in1=gcb, op=ALU.mult),
                      r=(("ST", l),) + tuple(("gC", j) for j in range(8)), w=(("ST", l),))
                kb.op("act", lambda e: e.activation(out=STbl, in_=STl, func=AF.Copy), r=(("ST", l),),
                      w=(("STb", l),))

            doneA = [False] * NCH
            doneB = [False] * NCH

            def thrA():
                for c in range(NCH):
                    if c >= 2:
                        kb.wait_until(lambda: doneB[c - 2])
                    stageA(c)
                    doneA[c] = True

            def thrB():
                for c in range(NCH):
                    kb.wait_until(lambda: doneA[c])
                    stageB(c)
                    doneB[c] = True
            psmap.update(PS_SCAN)
            kb.run_threads([thrA, thrB, gates_pool])

            yres = tuple(("ybuf", c) for c in range(NCH))

            def gnchain(k4):
                si = k4 % 2
                tp = tps[si]
                ng2, nsq, nt = ("g2", "sq", "t") if k4 < 2 else ("g1", "kk", "rn")
                R_ = lambda n: tr(si, n)
                for j in range(k4, 8, 4):
                    cj = l * 8 + j
                    m, mres = next_misc()
                    kb.group("pe", [lambda e: e.matmul(m, bones[:], ybuf[:, j, :], start=True, stop=True)],
                             r=("bones",) + yres, w=(mres,))
                    kb.op("dve", lambda e: e.scalar_tensor_tensor(out=tp[ng2][:], in0=m, scalar=-1.0 / 64,
                                                                  in1=ybuf[:, j, :], op0=ALU.mult, op1=ALU.add),
                          r=(mres,) + yres, w=(R_(ng2),))
                    kb.op("act", lambda e: e.activation(out=tp[nsq][:], in_=tp[ng2][:], func=AF.Square),
                          r=(R_(ng2),), w=(R_(nsq),))
                    m, mres = bsum(tp[nsq][:], (R_(nsq),))
                    kb.op("act", lambda e: e.activation(out=tp[nt][:], in_=m, func=AF.Ln, bias=CGN,
                                                        scale=1.0 / 64), r=(mres, "cst"), w=(R_(nt),), tab="exp")
                    kb.op("act", lambda e: e.activation(out=tp[nt][:], in_=tp[nt][:], func=AF.Exp, scale=-0.5),
                          r=(R_(nt),), w=(R_(nt),), tab="exp")
                    kb.op("dve", lambda e: e.tensor_tensor(out=tp[ng2][:], in0=tp[ng2][:], in1=tp[nt][:],
                                                           op=ALU.mult), r=(R_(ng2), R_(nt)), w=(R_(ng2),))
                    kb.op("dve", lambda e: e.tensor_scalar(out=tp[ng2][:], in0=tp[ng2][:], scalar1=V("ln_w", cj),
                                                           scalar2=V("ln_b", cj), op0=ALU.mult, op1=ALU.add),
                          r=(R_(ng2), "vecs"), w=(R_(ng2),))
                    kb.op("pool", lambda e: e.tensor_tensor(out=tp[ng2][:], in0=tp[ng2][:], in1=vb[:, j, :],
                                                            op=ALU.add), r=(R_(ng2), ("vb", j)), w=(R_(ng2),))
                    kb.op("pool", lambda e: e.tensor_tensor(out=yr[:, j, :], in0=tp[ng2][:], in1=sg[:, j, :],
                                                            op=ALU.mult), r=(R_(ng2), ("sg", j)), w=(("yr", j),))
            psmap.update(PS4)
            kb.run_threads([lambda: gnchain(0), lambda: gnchain(1), lambda: gnchain(2), lambda: gnchain(3)])

            ymres = tuple(("yr", j) for j in range(8)) + tuple(("ypool", q) for q in range(8))
            yrhs = lambda kc: (yr[:, kc, :] if kc < 8 else ypool[:, kc - 8, :])
            for d in range(16):
                acc, ares = mm_proj(l, "out", d, yrhs, ymres)
                kb.op("dve", lambda e: e.tensor_tensor(out=xT[:, d, :], in0=acc, in1=xT[:, d, :], op=ALU.add),
                      r=(ares, ("xT", d)), w=(("xT", d),))
                kb.op("act", lambda e: e.activation(out=actbf[:, d, :], in_=xT[:, d, :], func=AF.Copy),
                      r=(("xT", d),), w=(("actbf", d),))

            for pe_ in range(8):
                wple_t, wple_res, wple_n = next_w(l, "ple", pe_)
                wple = wple_t[:, 0:512].rearrange("p (a k c) -> p a k c", a=2, k=2)
                pm = []
                for dd in range(2):
                    m, mres = next_misc()
                    kb.group("pe", [lambda e, k=k: e.matmul(m, wple[:, dd, k, :], pbf[:, k, :], start=(k == 0),
                                                            stop=(k == 1)) for k in range(2)],
                             r=(wple_res, pbres), w=(mres,))
                    pm.append((m, mres))
                release_w(wple_n)
                for dd in range(2):
                    d = pe_ * 2 + dd
                    tp = tps[dd]
                    m, mres = pm[dd]
                    acc, ares = mm_proj(l, "pg", d, hrhs, hres)
                    kb.op("act", lambda e: e.activation(out=tp["g1"][:], in_=acc, func=AF.Sigmoid,
                                                        bias=V("b_pg", l * 16 + d), scale=1.0),
                          r=(ares, "vecs"), w=(tr(dd, "g1"),), tab="sig")
                    kb.op("dve", lambda e: e.tensor_tensor(out=tp["g2"][:], in0=m, in1=tp["g1"][:], op=ALU.mult),
                          r=(mres, tr(dd, "g1")), w=(tr(dd, "g2"),))
                    kb.op("pool", lambda e: e.tensor_tensor(out=xT[:, d, :], in0=xT[:, d, :], in1=tp["g2"][:],
                                                            op=ALU.add), r=(("xT", d), tr(dd, "g2")), w=(("xT", d),))

            if l == Ld - 1:
                rms_norm(l, lambda c: kb.op(
                    "dve", lambda e: e.scalar_tensor_tensor(out=xT[:, c, :], in0=xT[:, c, :],
                                                            scalar=V("final_g", c), in1=trstd[:],
                                                            op0=ALU.mult, op1=ALU.mult),
                    r=(("xT", c), "t_rstd", "vecs"), w=(("xT", c),)))
                kb.dma("sp", "xstore",
                       [lambda e, c4=c4: e.dma_start(out=out_d.rearrange("(c p) t -> p c t", p=128)[:, c4 * 4:c4 * 4 + 4, t0:t0 + TT],
                                                     in_=xT[:, c4 * 4:c4 * 4 + 4, :])
                        for c4 in range(4)],
                       r=tuple(("xT", c) for c in range(16)), w=("out",))

    if dry:
        return order_rec
    kb.final_wait("sp", ["out"])
    return nc, kb


def build2(T, Ld, **kw):
    order = build(T, Ld, worder=None, **kw)
    return build(T, Ld, worder=order, **kw)


def _prep_common(inputs, Ld):
    f = lambda a: np.asarray(a, dtype=np.float32)
    w_in, w_out, w_pg, w_ple = f(inputs["w_in"]), f(inputs["w_out"]), f(inputs["w_pg"]), f(inputs["w_ple"])
    wst = np.empty((Ld, 128, FW), np.float32)
    for l in range(Ld):
        parts = []
        wi = w_in[l].reshape(16, 128, NIN)
        for cb in range(49):
            parts.append(wi[:, :, cb * 128:(cb + 1) * 128].transpose(1, 0, 2).reshape(128, 2048))
        wo = w_out[l].reshape(16, 128, D)
        for d in range(16):
            parts.append(wo[:, :, d * 128:(d + 1) * 128].transpose(1, 0, 2).reshape(128, 2048))
        wg = w_pg[l].reshape(16, 128, D)
        for d in range(16):
            parts.append(wg[:, :, d * 128:(d + 1) * 128].transpose(1, 0, 2).reshape(128, 2048))
        wp = w_ple[l].reshape(2, 128, 16, 128)
        for e in range(8):
            parts.append(wp[:, :, e * 2:(e + 1) * 2, :].transpose(1, 2, 0, 3).reshape(128, 512))
        wst[l] = np.concatenate(parts, axis=1)
    smw = np.zeros((Ld, 128, SMW), np.float32)
    w_up, a_up, v_down, v_up, w_pool = (f(inputs[k]) for k in ["w_up", "a_up", "v_down", "v_up", "w_pool"])
    for l in range(Ld):
        smw[l, 0:64, 0:1024] = w_up[l]
        smw[l, 64:128, 0:1024] = a_up[l]
        if l > 0:
            smw[l, :, 1024:1280] = v_down[l - 1].reshape(8, 128, 32).transpose(1, 0, 2).reshape(128, 256)
            smw[l, 0:32, 1280:2304] = v_up[l - 1]
        smw[l, :, 2304:4352] = w_pool[l].reshape(4, 2, 128, 256).transpose(2, 0, 1, 3).reshape(128, 2048)
    voff, NV = vec_layout(Ld)
    vecs = np.zeros((128, NV), np.float32)

    def put(name, arr, per):
        a = f(arr).reshape(-1, per, 128).transpose(2, 0, 1).reshape(128, -1)
        vecs[:, voff[name]:voff[name] + a.shape[1]] = a
    put("norm_g", inputs["norm_g"][:Ld], 16)
    put("mu", inputs["mu"][:Ld], 25)
    for nm in ["w0", "a0", "k_k", "k_a", "ln_w", "ln_b", "pool_scale"]:
        put(nm, inputs[nm][:Ld], 8)
    put("r_k", f(inputs["r_k"])[:Ld].reshape(Ld, 1024), 8)
    if Ld > 1:
        put("v0", inputs["v0"][:Ld - 1], 8)
    put("b_pg", inputs["b_pg"][:Ld], 16)
    put("final_g", f(inputs["final_g"]).reshape(1, D), 16)
    return wst, smw, vecs


NOSELF = ("pe",)
_CACHE = {}
STOP = 99


def kernel(**inputs):
    x = np.asarray(inputs["x"], dtype=np.float32)
    p = np.asarray(inputs["p"], dtype=np.float32)
    B, T, _ = x.shape
    Ld = p.shape[0]
    assert B == 8 and T % TT == 0
    wst, smw, vecs = _prep_common(inputs, Ld)
    key = (T, Ld)
    if key not in _CACHE:
        _CACHE[key] = build2(T, Ld)[0]
    nc = _CACHE[key]
    in_maps = []
    for b in range(B):
        in_maps.append({
            "xT": np.ascontiguousarray(x[b].T),
            "pT": np.ascontiguousarray(p[:, b].transpose(0, 2, 1)),
            "wst": wst, "smw": smw, "vecs": vecs,
        })
    res = run_bass_kernel_spmd(nc, in_maps, core_ids=list(range(B)))
    out = np.stack([np.asarray(r["outT"]).T for r in res.results], axis=0)
    return np.ascontiguousarray(out.astype(np.float32))
```

```python
import threading
import numpy as np
import concourse.bass as bass
import concourse.mybir as mybir
from concourse.bass_utils import run_bass_kernel_spmd
from concourse.alu_op_type import AluOpType as ALU

F32 = mybir.dt.float32
BF16 = mybir.dt.bfloat16
I32 = mybir.dt.int32
AF = mybir.ActivationFunctionType

D = 2048
NIN = 6272
TT = 256
C = 64
NCH = TT // C
RING = 4
CHW = 2048
FW = 81 * 2048 + 4 * 1024
SMW = 4352
RMS_EPS = 1e-6
GN_EPS = 64e-5
EH = float(np.exp(-0.5))

def vec_layout(Ld):
    off = {}
    n = 0
    for nm, cnt in [("norm_g", Ld * 16), ("mu", Ld * 25), ("w0", Ld * 8), ("a0", Ld * 8), ("k_k", Ld * 8),
                    ("k_a", Ld * 8), ("ln_w", Ld * 8), ("ln_b", Ld * 8), ("r_k", Ld * 8), ("pool_scale", Ld * 8),
                    ("v0", max(Ld - 1, 1) * 8), ("b_pg", Ld * 16), ("final_g", 16)]:
        off[nm] = n
        n += cnt
    return off, n


class KB:
    def __init__(self, nc):
        self.nc = nc
        self.E = {"pe": nc.tensor, "dve": nc.vector, "act": nc.scalar, "pool": nc.gpsimd, "sp": nc.sync}
        self.semh = {}
        for k in self.E:
            self.semh[k] = nc.semaphore("s_" + k).__enter__()
        self.cnt = {k: 0 for k in self.E}
        self.clock = {k: {} for k in self.E}
        self.res = {}
        self.dcnt = {}
        self.nwait = 0
        self.enabled = True
        self.noself = NOSELF
        self._thr_active = False
        self._tl = threading.local()
        self.efree = {}
        self.evt_fin = {}
        self.cnt_model = {}
        self.dry_commit = True

    def _deps(self, r, w):
        d = {}
        for n in r:
            st = self.res.get(n)
            if st and st[0] is not None:
                k, v = st[0]
                if d.get(k, 0) < v:
                    d[k] = v
        for n in w:
            st = self.res.get(n)
            if st:
                if st[0] is not None:
                    k, v = st[0]
                    if d.get(k, 0) < v:
                        d[k] = v
                for k, v in st[1].items():
                    if d.get(k, 0) < v:
                        d[k] = v
        return d

    def _wait(self, eng, d):
        ck = self.clock[eng]
        for k, v in d.items():
            if k == eng and eng in self.noself:
                continue
            if ck.get(k, 0) < v:
                self.E[eng].wait_ge(self.semh[k], v)
                ck[k] = v
                self.nwait += 1

    def _commit(self, ev, r, w):
        k, v = ev
        for n in r:
            st = self.res.setdefault(n, [None, {}])
            if st[1].get(k, 0) < v:
                st[1][k] = v
        for n in w:
            self.res[n] = [ev, {}]

    def run_threads(self, fns):
        fns = [f for f in fns if f is not None]
        n = len(fns)
        if n == 0:
            return
        if n == 1:
            fns[0]()
            return
        assert not self._thr_active
        self._thr_active = True
        self._cond = threading.Condition()
        self._turn = 0
        self._alive = [True] * n
        self._pend = [("start",)] * n
        self._last = 0
        self._exc = None

        def worker(i):
            self._tl.i = i
            with self._cond:
                while self._turn != i:
                    self._cond.wait()
                self._pend[i] = None
            try:
                fns[i]()
            except BaseException as e:
                self._exc = e
            finally:
                with self._cond:
                    self._alive[i] = False
                    self._pend[i] = None
                    self._pick()
                    self._cond.notify_all()
        ths = [threading.Thread(target=worker, args=(i,)) for i in range(n)]
        for t in ths:
            t.start()
        for t in ths:
            t.join()
        self._thr_active = False
        self._tl.i = None
        if self._exc is not None:
            raise self._exc

    def _est_start(self, eng, r, w):
        rdy = 0.0
        for k, v in self._deps(r, w).items():
            f = self.evt_fin.get((k, v), 0.0) + 150.0
            if f > rdy:
                rdy = f
        return max(self.efree.get(eng, 0.0), rdy)

    def _pick(self):
        n = len(self._alive)
        best, bk = None, None
        for d in range(1, n + 1):
            k = (self._last + d) % n
            if not self._alive[k]:
                continue
            p = self._pend[k]
            if p is None:
                continue
            if p[0] == "start":
                st = -2.0
            elif p[0] == "wait":
                if not p[1]():
                    continue
                st = -1.0
            else:
                st = self._est_start(p[1], p[2], p[3])
            if best is None or st < best:
                best, bk = st, k
        if bk is None:
            if any(self._alive):
                self._exc = RuntimeError("scheduler deadlock: all threads blocked in wait_until")
                for k in range(n):
                    if self._alive[k]:
                        bk = k
                        break
            else:
                self._turn = -1
                return
        self._turn = bk
        self._last = bk

    def _sched(self, desc):
        if not self._thr_active:
            return
        i = getattr(self._tl, "i", None)
        if i is None:
            return
        with self._cond:
            self._pend[i] = desc
            self._pick()
            self._cond.notify_all()
            while self._turn != i:
                self._cond.wait()
            self._pend[i] = None
        if self._exc is not None:
            raise self._exc

    def wait_until(self, pred):
        if pred():
            return
        assert self._thr_active, "wait_until outside threads would block forever"
        self._sched(("wait", pred))
        assert pred()

    def tid(self):
        if not self._thr_active:
            return 0
        i = getattr(self._tl, "i", None)
        return 0 if i is None else i

    DEFDUR = {"dve": 500.0, "act": 420.0, "pool": 650.0, "pe": 35.0, "sp": 100.0}

    def _model(self, eng, r, w, dur, ev, async_dur=None):
        st = self._est_start(eng, r, w)
        if async_dur is None:
            fin = st + dur
            self.efree[eng] = fin
        else:
            self.efree[eng] = st + dur
            fin = st + async_dur
        self.evt_fin[ev] = fin

    def op(self, eng, fn, r=(), w=(), dur=None):
        dur = self.DEFDUR[eng] if dur is None else dur
        self._sched(("op", eng, r, w))
        self.cnt_model[eng] = self.cnt_model.get(eng, 0) + 1
        if not self.enabled:
            self._model(eng, r, w, dur, (eng, self.cnt_model[eng]))
            self._commit((eng, self.cnt_model[eng]), r, w)
            return
        assert self.cnt[eng] + 1 == self.cnt_model[eng]
        self._model(eng, r, w, dur, (eng, self.cnt[eng] + 1))
        self._wait(eng, self._deps(r, w))
        inst = fn(self.E[eng])
        self.cnt[eng] += 1
        inst.then_inc(self.semh[eng], 1)
        self._commit((eng, self.cnt[eng]), r, w)

    def group(self, eng, fns, r=(), w=(), dur=None):
        dur = self.DEFDUR[eng] * len(fns) if dur is None else dur
        self._sched(("op", eng, r, w))
        self.cnt_model[eng] = self.cnt_model.get(eng, 0) + 1
        if not self.enabled:
            self._model(eng, r, w, dur, (eng, self.cnt_model[eng]))
            self._commit((eng, self.cnt_model[eng]), r, w)
            return
        assert self.cnt[eng] + 1 == self.cnt_model[eng]
        self._model(eng, r, w, dur, (eng, self.cnt[eng] + 1))
        self._wait(eng, self._deps(r, w))
        inst = None
        for fn in fns:
            inst = fn(self.E[eng])
        self.cnt[eng] += 1
        inst.then_inc(self.semh[eng], 1)
        self._commit((eng, self.cnt[eng]), r, w)

    def dma(self, q, slot, fns, r=(), w=(), sw=True):
        if sw:
            self._sched(("op", q, r, w))
        self.cnt_model[slot] = self.cnt_model.get(slot, 0) + 16 * len(fns)
        if not self.enabled:
            self._model(q, r, w, 100.0, (slot, self.cnt_model[slot]), async_dur=3000.0)
            self._commit((slot, self.cnt_model[slot]), r, w)
            return
        if slot not in self.semh:
            self.semh[slot] = self.nc.semaphore("d_" + slot).__enter__()
            self.dcnt[slot] = 0
        self._model(q, r, w, 100.0, (slot, self.dcnt[slot] + 16 * len(fns)), async_dur=3000.0)
        self._wait(q, self._deps(r, w))
        for fn in fns:
            fn(self.E[q]).then_inc(self.semh[slot], 16)
            self.dcnt[slot] += 16
        self._commit((slot, self.dcnt[slot]), r, w)

    def final_wait(self, eng, names):
        d = {}
        for n in names:
            st = self.res.get(n)
            if st:
                if st[0] is not None:
                    k, v = st[0]
                    d[k] = max(d.get(k, 0), v)
                for k, v in st[1].items():
                    d[k] = max(d.get(k, 0), v)
        self._wait(eng, d)


def wchunk(kind, i):
    if kind == "in":
        return i * 2048, 2048
    if kind == "out":
        return (49 + i) * 2048, 2048
    if kind == "pg":
        return (65 + i) * 2048, 2048
    if kind == "ple":
        return 81 * 2048 + i * 512, 512
    raise ValueError(kind)


def build(T, Ld, chain_f32=False, stop=99, worder=None):
    dry = worder is None
    NT = T // TT
    nc = bass.Bass("TRN2", target_bir_lowering=False)
    voff, NV = vec_layout(Ld)
    CHD = F32 if chain_f32 else BF16

    xT_d = nc.dram_tensor("xT", [D, T], F32, kind="ExternalInput").ap()
    pT_d = nc.dram_tensor("pT", [Ld, 256, T], F32, kind="ExternalInput").ap()
    wst_d = nc.dram_tensor("wst", [Ld, 128, FW], F32, kind="ExternalInput").ap()
    smw_d = nc.dram_tensor("smw", [Ld, 128, SMW], F32, kind="ExternalInput").ap()
    vecs_d = nc.dram_tensor("vecs", [128, NV], F32, kind="ExternalInput").ap()
    out_d = nc.dram_tensor("outT", [D, T], F32, kind="ExternalOutput").ap()
    wsb_d = nc.dram_tensor("wsb", [Ld, 128, FW], BF16, kind="Internal").ap()
    smb_d = nc.dram_tensor("smb", [Ld, 128, SMW], BF16, kind="Internal").ap()

    kb = KB(nc)
    if dry:
        kb.enabled = False

    def sb(name, shape, dt=F32):
        return nc.sbuf_tensor(name, shape, dt).__enter__()

    def ps(name, shape, dt=F32):
        return nc.psum_tensor(name, shape, dt).__enter__()

    xT = sb("xT_s", [128, 16, TT])
    actbf = sb("actbf", [128, 16, TT], BF16)
    yr = sb("yr", [128, 8, TT], BF16)
    ypool = sb("ypool", [128, 8, TT], BF16)
    ring = [sb("ring%d" % i, [128, CHW], BF16) for i in range(RING)]
    smw = sb("smw_s", [128, SMW], BF16)
    vecs = sb("vecs_s", [128, NV])
    nw0 = sb("nw0", [128, Ld * 8])
    cst = sb("cst", [128, 8])
    zraws = [sb("zraw%d" % i, [128, TT + 1]) for i in range(4)]
    vf = sb("vf", [128, 8, TT])
    vb = sb("vb", [128, 8, TT])
    vbf = sb("vbf", [128, 8, TT], BF16)
    AR = sb("AR", [128, 2, 8, TT], BF16)
    KT = sb("KT", [128, 8, TT], BF16)
    BT = sb("BT", [128, 8, TT], BF16)
    Vt = sb("Vt", [128, 8, NCH, 64], BF16)
    Kt = sb("Kt", [128, 8, NCH, 64], BF16)
    Bt = sb("Bt", [128, 8, NCH, 64], BF16)
    sg = sb("sg", [128, 8, TT], BF16)
    ybuf = sb("ybuf", [128, 8, TT])
    ST = sb("ST", [128, Ld, 8, 64])
    STb = sb("STb", [128, Ld, 8, 64], BF16)
    carry = sb("carry", [128, Ld * 25])
    halo = sb("halo", [128, Ld, 8, 16])
    gC = sb("gC", [128, 8, NCH])
    loraIn = sb("loraIn", [128, TT], BF16)
    pbfs = [sb("pbf%d" % i, [128, 2, TT], BF16) for i in range(2)]
    t32 = sb("t32", [32, TT], BF16)
    ident = sb("ident", [128, 128], BF16)
    onesf = sb("onesf", [128, 128])
    bones = sb("bones", [128, 128])
    maskAR = sb("maskAR", [128, 2, 64])
    maskP = sb("maskP", [128, 64])
    identC = sb("identC", [128, 64])
    rmask = sb("rmask", [128, TT])
    invcnt = sb("invcnt", [128, 4, 16])
    iot = sb("iot", [128, 16], I32)
    TN = ["lw", "L", "eL", "eLm", "eLp", "icl", "r", "k", "kk", "sq", "rn", "kkn", "t", "kp", "d", "g1", "g2"]
    tps = []
    for si in range(2):
        dct = {n: sb("t%d_%s" % (si, n), [128, TT]) for n in TN}
        dct["tb"] = dct["lw"]
        dct["rk"] = dct["L"]
        tps.append(dct)
    ALIAS = {"tb": "lw", "rk": "L"}

    def tr(si, n):
        return ("t", si, ALIAS.get(n, n))
    tzs = sb("t_zs", [128, TT])
    trstd = sb("t_rstd", [128, TT])
    pg1 = sb("t_pg1", [128, TT])
    ptt = sb("t_pt", [128, 16])
    ub = [sb("ub%d" % i, [128, 16 + TT]) for i in range(2)]
    us = [sb("us%d" % i, [128, 16 + TT]) for i in range(2)]
    dpool = [sb("dpool%d" % i, [128, TT], BF16) for i in range(2)]
    ABRBs = [sb("ABRB%d" % i, [128, 8, 2, 64], BF16) for i in range(2)]
    AKRKs = [sb("AKRK%d" % i, [128, 8, 2, 64], BF16) for i in range(2)]
    Pb = [sb("Pb%d" % i, [128, 8, 64], CHD) for i in range(2)]
    Qb = [sb("Qb%d" % i, [128, 8, 64], CHD) for i in range(2)]
    Tb = [sb("Tb%d" % i, [128, 8, 64], CHD) for i in range(2)]
    TFs = [sb("TF%d" % i, [128, 8, 64], CHD) for i in range(2)]
    Wsb = sb("Wsb", [128, 8, 64], CHD)
    Usb = sb("Usb", [128, 8, 64], BF16)

    accp = [ps("accp%d" % i, [128, 512]) for i in range(2)]
    miscp = [ps("miscp%d" % i, [128, 512]) for i in range(2)]
    pA = ps("pA", [128, 1024])
    pX = [ps("pX%d" % i, [128, 512]) for i in range(2)]
    accbanks = [(accp[0], "accp0"), (accp[1], "accp1"), (pX[0], "pX0"), (pX[1], "pX1")]
    acc_i = [0]
    misc_i = [0]

    psmap = {"acc": None, "misc": None}

    def next_acc():
        if kb._thr_active:
            bank = psmap["acc"][kb.tid()]
            return bank[0][:, 0:TT], bank[1]
        i = acc_i[0] % 4
        acc_i[0] += 1
        return accbanks[i][0][:, 0:TT], accbanks[i][1]

    def next_misc():
        if kb._thr_active:
            bank = psmap["misc"][kb.tid()]
            return bank[0][:, 0:TT], bank[1]
        i = misc_i[0] % 2
        misc_i[0] += 1
        return miscp[i][:, 0:TT], "miscp%d" % i

    def V(name, idx):
        o = voff[name] + idx
        return vecs[:, o:o + 1]

    wst2 = wst_d.rearrange("l p (a c) -> (l p a) c", c=2048)
    wsb2 = wsb_d.rearrange("l p (a c) -> (l p a) c", c=2048)
    rows = Ld * 128 * (FW // 2048)
    step = 664
    fns = []
    for r0 in range(0, rows, step):
        r1 = min(rows, r0 + step)
        fns.append(lambda e, r0=r0, r1=r1: e.dma_start(out=wsb2[r0:r1, :], in_=wst2[r0:r1, :]))
    smw2 = smw_d.rearrange("l p (a c) -> (l p a) c", c=256)
    smb2 = smb_d.rearrange("l p (a c) -> (l p a) c", c=256)
    rows2 = Ld * 128 * (SMW // 256)
    for r0 in range(0, rows2, 1088):
        r1 = min(rows2, r0 + 1088)
        fns.append(lambda e, r0=r0, r1=r1: e.dma_start(out=smb2[r0:r1, :], in_=smw2[r0:r1, :]))
    kb.dma("pool", "prolog", fns, r=(), w=("wsb", "smb"))
    kb.dma("sp", "vecs", [lambda e: e.dma_start(out=vecs[:], in_=vecs_d[:, :])], w=("vecs",))

    T0 = tps[0]
    kb.op("dve", lambda e: e.memset(onesf[:], 1.0), w=("onesf",))
    kb.op("dve", lambda e: e.memset(bones[:], 0.0), w=("bones",))
    kb.op("dve", lambda e: e.memset(bones[0:64, 0:64], 1.0), w=("bones",))
    kb.op("dve", lambda e: e.memset(bones[64:128, 64:128], 1.0), w=("bones",))
    kb.op("dve", lambda e: e.memset(rmask[:], 1.0), w=("rmask",))
    kb.op("dve", lambda e: e.memset(rmask[:].rearrange("p (c s) -> p c s", s=64)[:, :, 0:1], 0.0), w=("rmask",))
    for i_ in range(2):
        kb.op("dve", lambda e: e.memset(us[i_][:], 0.0), w=(("us", i_),))
    kb.op("dve", lambda e: e.memset(carry[:], 0.0), w=tuple(("carry", i) for i in range(Ld * 25)))
    kb.op("dve", lambda e: e.memset(halo[:], 0.0), w=tuple(("halo", l_, q_) for l_ in range(Ld) for q_ in range(8)))
    kb.op("dve", lambda e: e.memset(ST[:], 0.0), w=tuple(("ST", l_) for l_ in range(Ld)))
    kb.op("dve", lambda e: e.memset(STb[:], 0.0), w=tuple(("STb", l_) for l_ in range(Ld)))
    for i, val in enumerate([0.0, 1.0, -0.5, RMS_EPS, GN_EPS, -1.0, 1e-18]):
        kb.op("dve", lambda e, i=i, val=val: e.memset(cst[:, i:i + 1], val), w=("cst",))
    C0, C1, CM05, CRMS, CGN = (cst[:, i:i + 1] for i in range(5))
    CTINY = cst[:, 6:7]
    kb.op("pool", lambda e: e.affine_select(out=T0["t"][:, 0:128], in_=onesf[:], pattern=[[1, 128]],
                                            compare_op=ALU.is_equal, fill=0.0, base=0, channel_multiplier=-1),
          r=("onesf",), w=(tr(0, "t"),))
    kb.op("dve", lambda e: e.tensor_copy(out=ident[:], in_=T0["t"][:, 0:128]), r=(tr(0, "t"),), w=("ident",))
    for hp in range(2):
        sl = slice(hp * 64, hp * 64 + 64)
        kb.op("pool", lambda e: e.affine_select(out=identC[sl, :], in_=onesf[sl, 0:64], pattern=[[1, 64]],
                                                compare_op=ALU.is_equal, fill=0.0, base=0, channel_multiplier=-1),
              r=("onesf",), w=("identC",))
        kb.op("pool", lambda e: e.affine_select(out=maskAR[sl, 0, :], in_=onesf[sl, 0:64], pattern=[[1, 64]],
                                                compare_op=ALU.is_gt, fill=0.0, base=0, channel_multiplier=-1),
              r=("onesf",), w=("maskAR",))
        kb.op("pool", lambda e: e.affine_select(out=maskAR[sl, 1, :], in_=onesf[sl, 0:64], pattern=[[1, 64]],
                                                compare_op=ALU.is_ge, fill=0.0, base=0, channel_multiplier=-1),
              r=("onesf",), w=("maskAR",))
        kb.op("pool", lambda e: e.affine_select(out=maskP[sl, :], in_=onesf[sl, 0:64], pattern=[[-1, 64]],
                                                compare_op=ALU.is_gt, fill=0.0, base=0, channel_multiplier=1),
              r=("onesf",), w=("maskP",))
    kb.op("pool", lambda e: e.iota(iot[:], pattern=[[1, 16]], base=1, channel_multiplier=0), w=("iot",))
    kb.op("dve", lambda e: e.tensor_copy(out=T0["t"][:, 0:16], in_=iot[:]), r=("iot",), w=(tr(0, "t"),))
    for g in range(4):
        kb.op("dve", lambda e, g=g: e.tensor_scalar(out=T0["d"][:, 0:16], in0=T0["t"][:, 0:16],
                                                    scalar1=float(2 ** (g + 1)), scalar2=None, op0=ALU.min),
              r=(tr(0, "t"),), w=(tr(0, "d"),))
        kb.op("dve", lambda e, g=g: e.reciprocal(out=invcnt[:, g, :], in_=T0["d"][:, 0:16]), r=(tr(0, "d"),),
              w=("invcnt",))
    kb.op("dve", lambda e: e.tensor_scalar(out=nw0[:], in0=vecs[:, voff["w0"]:voff["w0"] + Ld * 8], scalar1=-1.0,
                                           scalar2=None, op0=ALU.mult), r=("vecs",), w=("nw0",))

    order_rec = []
    stream = worder if worder is not None else []
    wpos = [0]
    wissued = [0]
    released = set()

    def issue_w():
        n = wissued[0]
        l, kind, i = stream[n]
        o, sz = wchunk(kind, i)
        slot = n % RING
        kb.dma("sp", "ring%d" % slot,
               [lambda e: e.dma_start(out=ring[slot][:, 0:sz], in_=wsb_d[l, :, o:o + sz])],
               r=("wsb",), w=(("ring", slot),), sw=False)
        wissued[0] += 1

    def pump():
        if dry:
            return
        while wissued[0] < len(stream) and (wissued[0] < RING or (wissued[0] - RING) in released):
            issue_w()

    def next_w(l, kind, i):
        n = wpos[0]
        wpos[0] += 1
        if dry:
            order_rec.append((l, kind, i))
        else:
            assert stream[n] == (l, kind, i), ("weight order mismatch", n, stream[n], (l, kind, i))
            pump()
            assert wissued[0] > n, "weight ring deadlock"
        slot = n % RING
        return ring[slot], ("ring", slot), n

    def release_w(n):
        released.add(n)
        pump()

    def mm_proj(l, kind, i, rhs_of, rhs_res, nk=16):
        wt, wres, wn = next_w(l, kind, i)
        acc, ares = next_acc()
        w3 = wt[:].rearrange("p (k c) -> p k c", c=128)
        kb.group("pe", [lambda e, kc=kc: e.matmul(acc, w3[:, kc, :], rhs_of(kc), start=(kc == 0), stop=(kc == nk - 1))
                        for kc in range(nk)], r=(wres,) + tuple(rhs_res), w=(ares,), dur=137.0 * nk)
        release_w(wn)
        return acc, ares

    def lerp(si, acc, ares, l, cc, out_ap, out_res, zi=None, dn="d", priv=None):
        ci = l * 25 + cc
        zi = si if zi is None else zi
        if priv is None:
            zraw = zraws[zi]
            zr, zr0 = ("zraw", zi), ("zraw0", zi)
            td = tps[si][dn]
            tdr = tr(si, dn)
        else:
            zraw, zr, zr0, td, tdr = priv
        kb.op("act", lambda e: e.activation(out=zraw[:, 1:TT + 1], in_=acc, func=AF.Copy), r=(ares,), w=(zr,))
        kb.op("pool", lambda e: e.tensor_copy(out=zraw[:, 0:1], in_=carry[:, ci:ci + 1]), r=(("carry", ci),),
              w=(zr0,))
        kb.op("pool", lambda e: e.tensor_tensor(out=td[:], in0=zraw[:, 0:TT], in1=zraw[:, 1:TT + 1],
                                                op=ALU.subtract), r=(zr, zr0), w=(tdr,))
        kb.op("dve", lambda e: e.scalar_tensor_tensor(out=out_ap, in0=td[:], scalar=V("mu", ci),
                                                      in1=zraw[:, 1:TT + 1], op0=ALU.mult, op1=ALU.add),
              r=(tdr, zr, "vecs"), w=(out_res,))
        kb.op("pool", lambda e: e.tensor_copy(out=carry[:, ci:ci + 1], in_=zraw[:, TT:TT + 1]), r=(zr,),
              w=(("carry", ci),))

    def bsum(src_ap, src_res):
        m, mres = next_misc()
        kb.group("pe", [lambda e: e.matmul(m, bones[:], src_ap, start=True, stop=True)],
                 r=("bones",) + tuple(src_res), w=(mres,), dur=250.0)
        return m, mres

    hsl = [slice(0, 64), slice(64, 128)]
    PS2 = {"acc": [(accp[0], "accp0"), (accp[1], "accp1")], "misc": [(miscp[0], "miscp0"), (miscp[1], "miscp1")]}
    pA_lo, pA_hi = pA[:, 0:512], pA[:, 512:1024]
    PS2C = {"acc": [(accp[0], "accp0"), (accp[1], "accp1"), (miscp[0], "miscp0"), (miscp[1], "miscp1"), (pX[0], "pX0")],
            "misc": [(accp[0], "accp0"), (accp[1], "accp1"), (miscp[0], "miscp0"), (miscp[1], "miscp1"), (pX[1], "pX1")]}
    PS4 = {"acc": [None] * 4, "misc": [(miscp[0], "miscp0"), (miscp[1], "miscp1"), (accp[0], "accp0"), (accp[1], "accp1")]}
    PS_SCAN = {"acc": [None, None, (accp[1], "accp1")], "misc": [None, None, (miscp[1], "miscp1")]}

    def rms_norm(l_unused, write_fn):
        m, mres = next_misc()
        for c in range(16):
            sqn = ["sq", "kk", "rn"][c % 3]
            kb.op("act", lambda e: e.activation(out=T0[sqn][:], in_=xT[:, c, :], func=AF.Square),
                  r=(("xT", c),), w=(tr(0, sqn),))
            kb.group("pe", [lambda e: e.matmul(m, onesf[:], T0[sqn][:], start=(c == 0), stop=(c == 15))],
                     r=("onesf", tr(0, sqn)), w=(mres,))
        kb.op("act", lambda e: e.activation(out=T0["t"][:], in_=m, func=AF.Ln, bias=CRMS, scale=1.0 / D),
              r=(mres, "cst"), w=(tr(0, "t"),))
        kb.op("act", lambda e: e.activation(out=trstd[:], in_=T0["t"][:], func=AF.Exp, scale=-0.5),
              r=(tr(0, "t"),), w=("t_rstd",))
        for c in range(16):
            write_fn(c)

    for ti in range(NT):
        t0 = ti * TT
        for l in range(Ld):
            if l == 0:
                kb.dma("sp", "xload",
                       [lambda e, c4=c4: e.dma_start(out=xT[:, c4 * 4:c4 * 4 + 4, :],
                                                     in_=xT_d.rearrange("(c p) t -> p c t", p=128)[:, c4 * 4:c4 * 4 + 4, t0:t0 + TT])
                        for c4 in range(4)],
                       w=tuple(("xT", c) for c in range(16)))
            kb.dma("sp", "smw", [lambda e: e.dma_start(out=smw[:], in_=smb_d[l, :, :])], r=("smb",), w=("smw",))
            pbi = (ti * Ld + l) % 2
            pbf = pbfs[pbi]
            pbres = ("pbf", pbi)
            kb.dma("pool", "pload%d" % pbi,
                   [lambda e: e.dma_start(out=pbf[:], in_=pT_d[l].rearrange("(k q) t -> q k t", q=128)[:, :, t0:t0 + TT])],
                   w=(pbres,))
            lora = smw[:, 0:1024]
            vdn = smw[:, 1024:1280].rearrange("p (k c) -> p k c", c=32)
            vup = smw[:, 1280:2304]
            wpool = smw[:, 2304:4352].rearrange("p (g e d) -> p g e d", g=4, e=2)

            rms_norm(l, lambda c: kb.op(
                "dve", lambda e: e.scalar_tensor_tensor(out=actbf[:, c, :], in0=xT[:, c, :],
                                                        scalar=V("norm_g", l * 16 + c), in1=trstd[:],
                                                        op0=ALU.mult, op1=ALU.mult),
                r=(("xT", c), "t_rstd", "vecs"), w=(("actbf", c),)))
            hres = tuple(("actbf", c) for c in range(16))
            hrhs = lambda kc: actbf[:, kc, :]

            acc, ares = mm_proj(l, "in", 24, hrhs, hres)
            lerp(0, acc, ares, l, 24, tzs[:], "t_zs")
            kb.op("act", lambda e: e.activation(out=loraIn[0:64, :], in_=tzs[0:64, :], func=AF.Tanh),
                  r=("t_zs",), w=("loraIn0",))
            kb.op("act", lambda e: e.activation(out=loraIn[64:128, :], in_=tzs[64:128, :], func=AF.Copy),
                  r=("t_zs",), w=("loraIn1",))

            vsrc = vf if l == 0 else vb
            vname = "vf" if l == 0 else "vb"
            pTr = pA[:].bitcast(BF16).rearrange("p (j c k) -> p j c k", j=8, c=NCH)

            def to_tokmajor(src, src_res_fn, dst, dst_res):
                fl = []
                for j in range(8):
                    for c in range(NCH):
                        for hp in range(2):
                            fl.append(lambda e, j=j, c=c, hp=hp: e.transpose(
                                pTr[hsl[hp], j, c, :], src[hsl[hp], j, c * 64:(c + 1) * 64],
                                ident[hsl[hp], hp * 64:hp * 64 + 64]))
                kb.group("pe", fl, r=("ident",) + tuple(src_res_fn(j) for j in range(8)), w=("pA0", "pA1"),
                         dur=4800.0)
                kb.op("act", lambda e: e.activation(out=dst[:], in_=pTr, func=AF.Copy), r=("pA0", "pA1"), w=(dst_res,),
                      dur=2000.0)

            vdone = [False] * 8
            vpriv = (us[0], ("us", 0), ("us", 0), tzs, "t_zs")

            def thrV():
                for j in range(8):
                    acc, ares = mm_proj(l, "in", 16 + j, hrhs, hres)
                    lerp(0, acc, ares, l, 16 + j, vsrc[:, j, :], (vname, j), priv=vpriv)
                    kb.op("act", lambda e: e.activation(out=vbf[:, j, :], in_=vsrc[:, j, :], func=AF.Copy),
                          r=((vname, j),), w=(("vbf", j),))
                    if l == 0:
                        vdone[j] = True
                if l > 0:
                    m, mres = next_misc()
                    kb.group("pe", [lambda e, j=j: e.matmul(m[0:32, :], vdn[:, j, :], vbf[:, j, :], start=(j == 0),
                                                            stop=(j == 7)) for j in range(8)],
                             r=("smw",) + tuple(("vbf", j) for j in range(8)), w=(mres,), dur=500.0)
                    kb.op("act", lambda e: e.activation(out=t32[:], in_=m[0:32, :], func=AF.Copy), r=(mres,),
                          w=("t32",))
                    for j in range(8):
                        m, mres = next_misc()
                        kb.group("pe", [lambda e: e.matmul(m, vup[0:32, j * 128:(j + 1) * 128], t32[:], start=True,
                                                           stop=True)], r=("smw", "t32"), w=(mres,), dur=120.0)
                        kb.op("act", lambda e: e.activation(out=trstd[:], in_=m, func=AF.Sigmoid,
                                                            bias=V("v0", (l - 1) * 8 + j), scale=1.0),
                              r=(mres, "vecs"), w=("t_rstd",))
                        kb.op("pool", lambda e: e.tensor_tensor(out=pg1[:], in0=vf[:, j, :], in1=vb[:, j, :],
                                                                op=ALU.subtract), r=(("vf", j), ("vb", j)),
                              w=("t_pg1",))
                        kb.op("dve", lambda e: e.tensor_tensor(out=pg1[:], in0=pg1[:], in1=trstd[:],
                                                               op=ALU.mult), r=("t_pg1", "t_rstd"), w=("t_pg1",))
                        kb.op("pool", lambda e: e.tensor_tensor(out=vb[:, j, :], in0=vb[:, j, :], in1=pg1[:],
                                                                op=ALU.add), r=(("vb", j), "t_pg1"), w=(("vb", j),))
                        kb.op("act", lambda e: e.activation(out=vbf[:, j, :], in_=vb[:, j, :], func=AF.Copy),
                              r=(("vb", j),), w=(("vbf", j),))
                        vdone[j] = True
                to_tokmajor(vbf, lambda j: ("vbf", j), Vt, "Vt")

            doneX = [False] * 8
            doneY = [False] * 8

            def thrX(si):
                tp = tps[si]
                R_ = lambda n: tr(si, n)
                for j in range(si, 8, 2):
                    cj = l * 8 + j
                    if j >= 2:
                        kb.wait_until(lambda: doneY[j - 2])
                    m, mres = next_misc()
                    kb.group("pe", [lambda e: e.matmul(m, lora[0:64, j * 128:(j + 1) * 128], loraIn[0:64, :],
                                                       start=True, stop=True)], r=("smw", "loraIn0"), w=(mres,))
                    kb.op("act", lambda e: e.activation(out=tp["lw"][:], in_=m, func=AF.Sigmoid, bias=V("w0", cj),
                                                        scale=1.0), r=(mres, "vecs"), w=(R_("lw"),))
                    kb.op("dve", lambda e: e.tensor_tensor_scan(out=tp["L"][:], data0=rmask[:], data1=tp["lw"][:],
                                                                initial=0.0, op0=ALU.mult, op1=ALU.add),
                          r=("rmask", R_("lw")), w=(R_("L"),))
                    kb.op("act", lambda e: e.activation(out=tp["eL"][:], in_=tp["L"][:], func=AF.Exp, scale=-EH),
                          r=(R_("L"),), w=(R_("eL"),))
                    kb.op("act", lambda e: e.activation(out=tp["eLm"][:], in_=tp["L"][:], func=AF.Exp, scale=EH),
                          r=(R_("L"),), w=(R_("eLm"),))
                    kb.op("pool", lambda e: e.tensor_tensor(out=tp["t"][:], in0=tp["L"][:], in1=tp["lw"][:],
                                                            op=ALU.subtract), r=(R_("L"), R_("lw")), w=(R_("t"),))
                    kb.op("act", lambda e: e.activation(out=tp["eLp"][:], in_=tp["t"][:], func=AF.Exp, scale=-EH),
                          r=(R_("t"),), w=(R_("eLp"),))
                    kb.op("pool", lambda e: e.tensor_copy(
                        out=gC[:, j, :], in_=tp["eL"][:].rearrange("p (c s) -> p c s", s=64)[:, :, 63]),
                        r=(R_("eL"),), w=(("gC", j),))
                    m, mres = next_misc()
                    kb.group("pe", [lambda e: e.matmul(m, lora[64:128, j * 128:(j + 1) * 128], loraIn[64:128, :],
                                                       start=True, stop=True)], r=("smw", "loraIn1"), w=(mres,))
                    kb.op("act", lambda e: e.activation(out=tp["icl"][:], in_=m, func=AF.Sigmoid, bias=V("a0", cj),
                                                        scale=1.0), r=(mres, "vecs"), w=(R_("icl"),))
                    acc, ares = mm_proj(l, "in", j, hrhs, hres)
                    lerp(si, acc, ares, l, j, tp["r"][:], R_("r"))
                    kb.op("pool", lambda e: e.tensor_tensor(out=AR[:, 1, j, :], in0=tp["r"][:], in1=tp["eL"][:],
                                                           op=ALU.mult), r=(R_("r"), R_("eL")), w=(("RT", j),))
                    doneX[j] = True

            def thrY(si):
                tp = tps[si]
                R_ = lambda n: tr(si, n)
                for j in range(si, 8, 2):
                    cj = l * 8 + j
                    acc, ares = mm_proj(l, "in", 8 + j, hrhs, hres)
                    lerp(si, acc, ares, l, 8 + j, tp["k"][:], R_("k"), zi=2 + si, dn="sq")
                    kb.op("dve", lambda e: e.tensor_scalar(out=tp["kk"][:], in0=tp["k"][:], scalar1=V("k_k", cj),
                                                           scalar2=None, op0=ALU.mult), r=(R_("k"), "vecs"),
                          w=(R_("kk"),))
                    kb.op("act", lambda e: e.activation(out=tp["sq"][:], in_=tp["kk"][:], func=AF.Square),
                          r=(R_("kk"),), w=(R_("sq"),))
                    m, mres = bsum(tp["sq"][:], (R_("sq"),))
                    kb.op("act", lambda e: e.activation(out=tp["rn"][:], in_=m, func=AF.Ln, bias=CTINY, scale=1.0),
                          r=(mres, "cst"), w=(R_("rn"),))
                    kb.op("act", lambda e: e.activation(out=tp["rn"][:], in_=tp["rn"][:], func=AF.Exp, scale=-0.5),
                          r=(R_("rn"),), w=(R_("rn"),))
                    kb.op("pool", lambda e: e.tensor_tensor(out=tp["kkn"][:], in0=tp["kk"][:], in1=tp["rn"][:],
                                                           op=ALU.mult), r=(R_("kk"), R_("rn")), w=(R_("kkn"),))
                    kb.wait_until(lambda: doneX[j])
                    kb.op("dve", lambda e: e.tensor_scalar(out=tp["g1"][:], in0=tp["icl"][:], scalar1=-1.0,
                                                           scalar2=V("k_a", cj), op0=ALU.add, op1=ALU.mult),
                          r=(R_("icl"), "vecs"), w=(R_("g1"),))
                    kb.op("dve", lambda e: e.scalar_tensor_tensor(out=tp["kp"][:], in0=tp["g1"][:], scalar=1.0,
                                                                  in1=tp["k"][:], op0=ALU.add, op1=ALU.mult),
                          r=(R_("g1"), R_("k")), w=(R_("kp"),))
                    kb.op("dve", lambda e: e.tensor_tensor(out=KT[:, j, :], in0=tp["kp"][:], in1=tp["eLm"][:],
                                                           op=ALU.mult), r=(R_("kp"), R_("eLm")), w=(("KT", j),))
                    kb.op("dve", lambda e: e.scalar_tensor_tensor(out=AR[:, 0, j, :], in0=tp["kkn"][:], scalar=-1.0,
                                                                  in1=tp["eLp"][:], op0=ALU.mult, op1=ALU.mult),
                          r=(R_("kkn"), R_("eLp")), w=(("AT", j),))
                    kb.op("pool", lambda e: e.tensor_tensor(out=tp["g2"][:], in0=tp["kkn"][:], in1=tp["icl"][:],
                                                           op=ALU.mult), r=(R_("kkn"), R_("icl")), w=(R_("g2"),))
                    kb.op("pool", lambda e: e.tensor_tensor(out=BT[:, j, :], in0=tp["g2"][:], in1=tp["eLm"][:],
                                                           op=ALU.mult), r=(R_("g2"), R_("eLm")), w=(("BT", j),))
                    kb.op("dve", lambda e: e.scalar_tensor_tensor(out=tp["kk"][:], in0=tp["r"][:],
                                                                  scalar=V("r_k", cj), in1=tp["kp"][:],
                                                                  op0=ALU.mult, op1=ALU.mult),
                          r=(R_("r"), R_("kp"), "vecs"), w=(R_("kk"),))
                    m, mres = bsum(tp["kk"][:], (R_("kk"),))
                    kb.wait_until(lambda: vdone[j])
                    kb.op("dve", lambda e: e.tensor_tensor(out=vb[:, j, :], in0=m, in1=vsrc[:, j, :], op=ALU.mult),
                          r=(mres, (vname, j)), w=(("vb", j),))

                    doneY[j] = True

            def gates_pool():
                for j in range(8):
                    acc, ares = mm_proj(l, "in", 25 + j, hrhs, hres)
                    kb.op("act", lambda e: e.activation(out=sg[:, j, :], in_=acc, func=AF.Silu), r=(ares,),
                          w=(("sg", j),))
                for g in range(4):
                    win = 2 ** (g + 1)
                    for e_ in range(2):
                        q = 2 * g + e_
                        acc, ares = mm_proj(l, "in", 33 + q, hrhs, hres)
                        u = ub[e_]
                        kb.op("act", lambda e: e.activation(out=u[:, 16:16 + TT], in_=acc, func=AF.Copy), r=(ares,),
                              w=(("ub", e_),))
                        kb.op("pool", lambda e: e.tensor_copy(out=u[:, 0:16], in_=halo[:, l, q, :]),
                              r=(("halo", l, q),), w=(("ubh", e_),))
                        cur, curres = u, (("ub", e_), ("ubh", e_))
                        stp = 1
                        k_ = 0
                        while stp < win:
                            dst = us[k_ % 2]
                            dres = ("us", k_ % 2)
                            kb.op("pool", lambda e: e.tensor_tensor(out=dst[:, stp:16 + TT], in0=cur[:, stp:16 + TT],
                                                                    in1=cur[:, 0:16 + TT - stp], op=ALU.add),
                                  r=curres, w=(dres,))
                            cur, curres = dst, (dres,)
                            stp *= 2
                            k_ += 1
                        kb.op("pool", lambda e: e.tensor_copy(out=halo[:, l, q, :], in_=u[:, TT:TT + 16]),
                              r=(("ub", e_), ("ubh", e_)), w=(("halo", l, q),))
                        kb.op("dve", lambda e: e.scalar_tensor_tensor(out=dpool[e_][:], in0=cur[:, 16:16 + TT],
                                                                      scalar=1.0 / win, in1=u[:, 16:16 + TT],
                                                                      op0=ALU.mult, op1=ALU.subtract),
                              r=curres + (("ub", e_),), w=(("dpool", e_),))
                        if ti == 0:
                            kb.op("dve", lambda e: e.tensor_tensor(out=ptt[:], in0=cur[:, 16:32],
                                                                   in1=invcnt[:, g, :], op=ALU.mult),
                                  r=curres + ("invcnt",), w=("t_pt",))
                            kb.op("dve", lambda e: e.tensor_tensor(out=dpool[e_][:, 0:16], in0=ptt[:],
                                                                   in1=u[:, 16:32], op=ALU.subtract),
                                  r=("t_pt", ("ub", e_)), w=(("dpool", e_),))
                    for od in range(2):
                        q = 2 * g + od
                        m, mres = next_misc()
                        kb.group("pe", [lambda e, e_=e_: e.matmul(m, wpool[:, g, e_, od * 128:(od + 1) * 128],
                                                                  dpool[e_][:], start=(e_ == 0), stop=(e_ == 1))
                                        for e_ in range(2)], r=("smw", ("dpool", 0), ("dpool", 1)), w=(mres,))
                        acc, ares = mm_proj(l, "in", 41 + q, hrhs, hres)
                        kb.op("act", lambda e: e.activation(out=pg1[:], in_=acc, func=AF.Silu), r=(ares,),
                              w=("t_pg1",))
                        kb.op("dve", lambda e: e.scalar_tensor_tensor(out=ypool[:, q, :], in0=m,
                                                                      scalar=V("pool_scale", l * 8 + q),
                                                                      in1=pg1[:], op0=ALU.mult, op1=ALU.mult),
                              r=(mres, "t_pg1", "vecs"), w=(("ypool", q),))

            psmap.update(PS2C)
            kb.run_threads([lambda: thrX(0), lambda: thrX(1), lambda: thrY(0), lambda: thrY(1), thrV])
            to_tokmajor(KT, lambda j: ("KT", j), Kt, "Kt")
            to_tokmajor(BT, lambda j: ("BT", j), Bt, "Bt")

            STl = ST[:, l, :, :]
            STbl = STb[:, l, :, :]
            allAT = tuple(("AT", j) for j in range(8))
            allRT = tuple(("RT", j) for j in range(8))
            allKT = tuple(("KT", j) for j in range(8))
            allBT = tuple(("BT", j) for j in range(8))
            pA4 = pA[:].rearrange("p (j a t) -> p j a t", j=8, a=2)
            pX3 = [pX[i][:].rearrange("p (j t) -> p j t", j=8) for i in range(2)]
            pW3 = accp[0][:].rearrange("p (j t) -> p j t", j=8)
            pU3 = accp[0][:].rearrange("p (j t) -> p j t", j=8)
            pY3 = miscp[0][:].rearrange("p (j t) -> p j t", j=8)
            pS3 = miscp[0][:].rearrange("p (j t) -> p j t", j=8)
            mAR = maskAR[:].unsqueeze(1).broadcast_to([128, 8, 2, 64])
            mP = maskP[:].unsqueeze(1).broadcast_to([128, 8, 64])
            iC = identC[:].unsqueeze(1).broadcast_to([128, 8, 64])
            JH = [(j, hp) for j in range(8) for hp in range(2)]

            def stageA(c):
                tok = slice(c * 64, (c + 1) * 64)
                ABRB, AKRK, TF = ABRBs[c % 2], AKRKs[c % 2], TFs[c % 2]
                rAB, rAK, rTF = ("ABRB", c % 2), ("AKRK", c % 2), ("TF", c % 2)
                kb.group("pe", [lambda e, j=j, hp=hp: e.matmul(pA4[hsl[hp], j, :, :], BT[hsl[hp], j, tok],
                                                               AR[hsl[hp], :, j, tok], start=True, stop=True)
                                for j, hp in JH], r=allBT + allAT + allRT, w=("pA0", "pA1"))
                kb.op("dve", lambda e: e.tensor_tensor(out=ABRB[:], in0=pA4, in1=mAR, op=ALU.mult),
                      r=("pA0", "pA1", "maskAR"), w=(rAB,))
                kb.group("pe", [lambda e, j=j, hp=hp: e.matmul(pX3[0][hsl[hp], j, :], AR[hsl[hp], 0, j, tok],
                                                               BT[hsl[hp], j, tok], start=True, stop=True)
                                for j, hp in JH], r=allBT + allAT, w=("pX0",))
                kb.op("dve", lambda e: e.tensor_tensor(out=Pb[0][:], in0=pX3[0], in1=mP, op=ALU.mult),
                      r=("pX0", "maskP"), w=("Pb0",))
                kb.group("pe", [lambda e, j=j, hp=hp: e.matmul(pA4[hsl[hp], j, :, :], KT[hsl[hp], j, tok],
                                                               AR[hsl[hp], :, j, tok], start=True, stop=True)
                                for j, hp in JH], r=allKT + allAT + allRT, w=("pA0", "pA1"))
                kb.op("act", lambda e: e.activation(out=Qb[0][:], in_=ABRB[:, :, 0, :], func=AF.Copy),
                      r=(rAB,), w=("Qb0",))
                kb.op("dve", lambda e: e.tensor_tensor(out=AKRK[:], in0=pA4, in1=mAR, op=ALU.mult),
                      r=("pA0", "pA1", "maskAR"), w=(rAK,))
                kb.op("dve", lambda e: e.tensor_tensor(out=Tb[0][:], in0=ABRB[:, :, 0, :], in1=iC, op=ALU.add),
                      r=(rAB, "identC"), w=("Tb0",))
                for lev in range(5):
                    a, b = lev % 2, (lev + 1) % 2
                    kb.group("pe", [lambda e, j=j, hp=hp: e.matmul(pX3[0][hsl[hp], j, :], Qb[a][hsl[hp], j, :],
                                                                   Pb[a][hsl[hp], j, :], start=True, stop=True)
                                    for j, hp in JH], r=("Qb%d" % a, "Pb%d" % a), w=("pX0",))
                    kb.op("dve", lambda e: e.tensor_copy(out=Pb[b][:], in_=pX3[0]), r=("pX0",), w=("Pb%d" % b,))
                    if lev < 4:
                        kb.group("pe", [lambda e, j=j, hp=hp: e.matmul(pX3[1][hsl[hp], j, :], Pb[a][hsl[hp], j, :],
                                                                       Qb[a][hsl[hp], j, :], start=True, stop=True)
                                        for j, hp in JH], r=("Qb%d" % a, "Pb%d" % a), w=("pX1",))
                        kb.op("act", lambda e: e.activation(out=Qb[b][:], in_=pX3[1], func=AF.Copy),
                              r=("pX1",), w=("Qb%d" % b,))
                    kb.group("pe", [lambda e, j=j, hp=hp: e.matmul(pA4[hsl[hp], j, 0, :], Pb[b][hsl[hp], j, :],
                                                                   Tb[a][hsl[hp], j, :], start=True, stop=True)
                                    for j, hp in JH], r=("Pb%d" % b, "Tb%d" % a), w=("pA0", "pA1"))
                    tdst, tdres = (Tb[b], "Tb%d" % b) if lev < 4 else (TF, rTF)
                    kb.op("dve", lambda e: e.tensor_tensor(out=tdst[:], in0=pA4[:, :, 0, :], in1=Tb[a][:],
                                                           op=ALU.add), r=("pA0", "pA1", "Tb%d" % a), w=(tdres,))

            def stageB(c):
                tok = slice(c * 64, (c + 1) * 64)
                ABRB, AKRK, TF = ABRBs[c % 2], AKRKs[c % 2], TFs[c % 2]
                rAB, rAK, rTF = ("ABRB", c % 2), ("AKRK", c % 2), ("TF", c % 2)
                fl = []
                for j, hp in JH:
                    fl.append(lambda e, j=j, hp=hp: e.matmul(pW3[hsl[hp], j, :], AR[hsl[hp], 0, j, tok],
                                                             STbl[hsl[hp], j, :], start=True, stop=False))
                    fl.append(lambda e, j=j, hp=hp: e.matmul(pW3[hsl[hp], j, :], AKRK[hsl[hp], j, 0, :],
                                                             Vt[hsl[hp], j, c, :], start=False, stop=True))
                kb.group("pe", fl, dur=33.0 * len(fl), r=allAT + (("STb", l), rAK, "Vt"), w=("accp0",))
                kb.op("act", lambda e: e.activation(out=Wsb[:], in_=pW3, func=AF.Copy), r=("accp0",), w=("Wsb",))
                kb.group("pe", [lambda e, j=j, hp=hp: e.matmul(pU3[hsl[hp], j, :], TF[hsl[hp], j, :],
                                                               Wsb[hsl[hp], j, :], start=True, stop=True)
                                for j, hp in JH], r=(rTF, "Wsb"), w=("accp0",))
                kb.op("act", lambda e: e.activation(out=Usb[:], in_=pU3, func=AF.Copy), r=("accp0",), w=("Usb",))
                fl = []
                for j, hp in JH:
                    o = pY3[hsl[hp], j, :]
                    fl.append(lambda e, j=j, hp=hp, o=o: e.matmul(o, STbl[hsl[hp], j, :], AR[hsl[hp], 1, j, tok],
                                                                  start=True, stop=False))
                    fl.append(lambda e, j=j, hp=hp, o=o: e.matmul(o, Usb[hsl[hp], j, :], ABRB[hsl[hp], j, 1, :],
                                                                  start=False, stop=False))
                    fl.append(lambda e, j=j, hp=hp, o=o: e.matmul(o, Vt[hsl[hp], j, c, :], AKRK[hsl[hp], j, 1, :],
                                                                  start=False, stop=True))
                kb.group("pe", fl, dur=33.0 * len(fl), r=allRT + (("STb", l), "Usb", rAB, rAK, "Vt"), w=("miscp0",))
                kb.op("act", lambda e: e.activation(out=ybuf[:, :, tok], in_=pY3, func=AF.Copy), r=("miscp0",),
                      w=(("ybuf", c),))
                fl = []
                for j, hp in JH:
                    o = pS3[hsl[hp], j, :]
                    fl.append(lambda e, j=j, hp=hp, o=o: e.matmul(o, Bt[hsl[hp], j, c, :], Usb[hsl[hp], j, :],
                                                                  start=True, stop=False))
                    fl.append(lambda e, j=j, hp=hp, o=o: e.matmul(o, Kt[hsl[hp], j, c, :], Vt[hsl[hp], j, c, :],
                                                                  start=False, stop=True))
                kb.group("pe", fl, dur=33.0 * len(fl), r=("Bt", "Kt", "Vt", "Usb"), w=("miscp0",))
                kb.op("dve", lambda e: e.tensor_tensor(out=STl, in0=pS3, in1=STl, op=ALU.add),
                      r=("miscp0", ("ST", l)), w=(("ST", l),))
                gcb = gC[:, :, c:c + 1].broadcast_to([128, 8, 64])
                kb.op("dve", lambda e: e.tensor_tensor(out=STl, in0=STl, in1=gcb, op=ALU.mult),
                      r=(("ST", l),) + tuple(("gC", j) for j in range(8)), w=(("ST", l),))
                kb.op("act", lambda e: e.activation(out=STbl, in_=STl, func=AF.Copy), r=(("ST", l),),
                      w=(("STb", l),))

            doneA = [False] * NCH
            doneB = [False] * NCH

            def thrA():
                for c in range(NCH):
                    if c >= 2:
                        kb.wait_until(lambda: doneB[c - 2])
                    stageA(c)
                    doneA[c] = True

            def thrB():
                for c in range(NCH):
                    kb.wait_until(lambda: doneA[c])
                    stageB(c)
                    doneB[c] = True
            psmap.update(PS_SCAN)
            kb.run_threads([thrA, thrB, gates_pool])

            yres = tuple(("ybuf", c) for c in range(NCH))

            def gnchain(k4):
                si = k4 % 2
                tp = tps[si]
                ng2, nsq, nt = ("g2", "sq", "t") if k4 < 2 else ("g1", "kk", "rn")
                R_ = lambda n: tr(si, n)
                for j in range(k4, 8, 4):
                    cj = l * 8 + j
                    m, mres = next_misc()
                    kb.group("pe", [lambda e: e.matmul(m, bones[:], ybuf[:, j, :], start=True, stop=True)],
                             r=("bones",) + yres, w=(mres,))
                    kb.op("dve", lambda e: e.scalar_tensor_tensor(out=tp[ng2][:], in0=m, scalar=-1.0 / 64,
                                                                  in1=ybuf[:, j, :], op0=ALU.mult, op1=ALU.add),
                          r=(mres,) + yres, w=(R_(ng2),))
                    kb.op("act", lambda e: e.activation(out=tp[nsq][:], in_=tp[ng2][:], func=AF.Square),
                          r=(R_(ng2),), w=(R_(nsq),))
                    m, mres = bsum(tp[nsq][:], (R_(nsq),))
                    kb.op("act", lambda e: e.activation(out=tp[nt][:], in_=m, func=AF.Ln, bias=CGN,
                                                        scale=1.0 / 64), r=(mres, "cst"), w=(R_(nt),))
                    kb.op("act", lambda e: e.activation(out=tp[nt][:], in_=tp[nt][:], func=AF.Exp, scale=-0.5),
                          r=(R_(nt),), w=(R_(nt),))
                    kb.op("dve", lambda e: e.tensor_tensor(out=tp[ng2][:], in0=tp[ng2][:], in1=tp[nt][:],
                                                           op=ALU.mult), r=(R_(ng2), R_(nt)), w=(R_(ng2),))
                    kb.op("dve", lambda e: e.tensor_scalar(out=tp[ng2][:], in0=tp[ng2][:], scalar1=V("ln_w", cj),
                                                           scalar2=V("ln_b", cj), op0=ALU.mult, op1=ALU.add),
                          r=(R_(ng2), "vecs"), w=(R_(ng2),))
                    kb.op("pool", lambda e: e.tensor_tensor(out=tp[ng2][:], in0=tp[ng2][:], in1=vb[:, j, :],
                                                            op=ALU.add), r=(R_(ng2), ("vb", j)), w=(R_(ng2),))
                    kb.op("pool", lambda e: e.tensor_tensor(out=yr[:, j, :], in0=tp[ng2][:], in1=sg[:, j, :],
                                                            op=ALU.mult), r=(R_(ng2), ("sg", j)), w=(("yr", j),))
            psmap.update(PS4)
            kb.run_threads([lambda: gnchain(0), lambda: gnchain(1), lambda: gnchain(2), lambda: gnchain(3)])

            ymres = tuple(("yr", j) for j in range(8)) + tuple(("ypool", q) for q in range(8))
            yrhs = lambda kc: (yr[:, kc, :] if kc < 8 else ypool[:, kc - 8, :])
            for d in range(16):
                acc, ares = mm_proj(l, "out", d, yrhs, ymres)
                kb.op("dve", lambda e: e.tensor_tensor(out=xT[:, d, :], in0=acc, in1=xT[:, d, :], op=ALU.add),
                      r=(ares, ("xT", d)), w=(("xT", d),))
                kb.op("act", lambda e: e.activation(out=actbf[:, d, :], in_=xT[:, d, :], func=AF.Copy),
                      r=(("xT", d),), w=(("actbf", d),))

            for pe_ in range(8):
                wple_t, wple_res, wple_n = next_w(l, "ple", pe_)
                wple = wple_t[:, 0:512].rearrange("p (a k c) -> p a k c", a=2, k=2)
                pm = []
                for dd in range(2):
                    m, mres = next_misc()
                    kb.group("pe", [lambda e, k=k: e.matmul(m, wple[:, dd, k, :], pbf[:, k, :], start=(k == 0),
                                                            stop=(k == 1)) for k in range(2)],
                             r=(wple_res, pbres), w=(mres,))
                    pm.append((m, mres))
                release_w(wple_n)
                for dd in range(2):
                    d = pe_ * 2 + dd
                    tp = tps[dd]
                    m, mres = pm[dd]
                    acc, ares = mm_proj(l, "pg", d, hrhs, hres)
                    kb.op("act", lambda e: e.activation(out=tp["g1"][:], in_=acc, func=AF.Sigmoid,
                                                        bias=V("b_pg", l * 16 + d), scale=1.0),
                          r=(ares, "vecs"), w=(tr(dd, "g1"),))
                    kb.op("dve", lambda e: e.tensor_tensor(out=tp["g2"][:], in0=m, in1=tp["g1"][:], op=ALU.mult),
                          r=(mres, tr(dd, "g1")), w=(tr(dd, "g2"),))
                    kb.op("pool", lambda e: e.tensor_tensor(out=xT[:, d, :], in0=xT[:, d, :], in1=tp["g2"][:],
                                                            op=ALU.add), r=(("xT", d), tr(dd, "g2")), w=(("xT", d),))

            if l == Ld - 1:
                rms_norm(l, lambda c: kb.op(
                    "dve", lambda e: e.scalar_tensor_tensor(out=xT[:, c, :], in0=xT[:, c, :],
                                                            scalar=V("final_g", c), in1=trstd[:],
                                                            op0=ALU.mult, op1=ALU.mult),
                    r=(("xT", c), "t_rstd", "vecs"), w=(("xT", c),)))
                kb.dma("sp", "xstore",
                       [lambda e, c4=c4: e.dma_start(out=out_d.rearrange("(c p) t -> p c t", p=128)[:, c4 * 4:c4 * 4 + 4, t0:t0 + TT],
                                                     in_=xT[:, c4 * 4:c4 * 4 + 4, :])
                        for c4 in range(4)],
                       r=tuple(("xT", c) for c in range(16)), w=("out",))

    if dry:
        return order_rec
    kb.final_wait("sp", ["out"])
    return nc, kb


def build2(T, Ld, **kw):
    order = build(T, Ld, worder=None, **kw)
    return build(T, Ld, worder=order, **kw)


def _prep_common(inputs, Ld):
    f = lambda a: np.asarray(a, dtype=np.float32)
    w_in, w_out, w_pg, w_ple = f(inputs["w_in"]), f(inputs["w_out"]), f(inputs["w_pg"]), f(inputs["w_ple"])
    wst = np.empty((Ld, 128, FW), np.float32)
    for l in range(Ld):
        parts = []
        wi = w_in[l].reshape(16, 128, NIN)
        for cb in range(49):
            parts.append(wi[:, :, cb * 128:(cb + 1) * 128].transpose(1, 0, 2).reshape(128, 2048))
        wo = w_out[l].reshape(16, 128, D)
        for d in range(16):
            parts.append(wo[:, :, d * 128:(d + 1) * 128].transpose(1, 0, 2).reshape(128, 2048))
        wg = w_pg[l].reshape(16, 128, D)
        for d in range(16):
            parts.append(wg[:, :, d * 128:(d + 1) * 128].transpose(1, 0, 2).reshape(128, 2048))
        wp = w_ple[l].reshape(2, 128, 16, 128)
        for e in range(8):
            parts.append(wp[:, :, e * 2:(e + 1) * 2, :].transpose(1, 2, 0, 3).reshape(128, 512))
        wst[l] = np.concatenate(parts, axis=1)
    smw = np.zeros((Ld, 128, SMW), np.float32)
    w_up, a_up, v_down, v_up, w_pool = (f(inputs[k]) for k in ["w_up", "a_up", "v_down", "v_up", "w_pool"])
    for l in range(Ld):
        smw[l, 0:64, 0:1024] = w_up[l]
        smw[l, 64:128, 0:1024] = a_up[l]
        if l > 0:
            smw[l, :, 1024:1280] = v_down[l - 1].reshape(8, 128, 32).transpose(1, 0, 2).reshape(128, 256)
            smw[l, 0:32, 1280:2304] = v_up[l - 1]
        smw[l, :, 2304:4352] = w_pool[l].reshape(4, 2, 128, 256).transpose(2, 0, 1, 3).reshape(128, 2048)
    voff, NV = vec_layout(Ld)
    vecs = np.zeros((128, NV), np.float32)

    def put(name, arr, per):
        a = f(arr).reshape(-1, per, 128).transpose(2, 0, 1).reshape(128, -1)
        vecs[:, voff[name]:voff[name] + a.shape[1]] = a
    put("norm_g", inputs["norm_g"][:Ld], 16)
    put("mu", inputs["mu"][:Ld], 25)
    for nm in ["w0", "a0", "k_k", "k_a", "ln_w", "ln_b", "pool_scale"]:
        put(nm, inputs[nm][:Ld], 8)
    put("r_k", f(inputs["r_k"])[:Ld].reshape(Ld, 1024), 8)
    if Ld > 1:
        put("v0", inputs["v0"][:Ld - 1], 8)
    put("b_pg", inputs["b_pg"][:Ld], 16)
    put("final_g", f(inputs["final_g"]).reshape(1, D), 16)
    return wst, smw, vecs


NOSELF = ("pe",)
_CACHE = {}
STOP = 99


def kernel(**inputs):
    x = np.asarray(inputs["x"], dtype=np.float32)
    p = np.asarray(inputs["p"], dtype=np.float32)
    B, T, _ = x.shape
    Ld = p.shape[0]
    assert B == 8 and T % TT == 0
    wst, smw, vecs = _prep_common(inputs, Ld)
    key = (T, Ld)
    if key not in _CACHE:
        _CACHE[key] = build2(T, Ld)[0]
    nc = _CACHE[key]
    in_maps = []
    for b in range(B):
        in_maps.append({
            "xT": np.ascontiguousarray(x[b].T),
            "pT": np.ascontiguousarray(p[:, b].transpose(0, 2, 1)),
            "wst": wst, "smw": smw, "vecs": vecs,
        })
    res = run_bass_kernel_spmd(nc, in_maps, core_ids=list(range(B)))
    out = np.stack([np.asarray(r["outT"]).T for r in res.results], axis=0)
    return np.ascontiguousarray(out.astype(np.float32))
```

```python
import threading
import numpy as np
import concourse.bass as bass
import concourse.mybir as mybir
from concourse.bass_utils import run_bass_kernel_spmd
from concourse.alu_op_type import AluOpType as ALU

F32 = mybir.dt.float32
BF16 = mybir.dt.bfloat16
I32 = mybir.dt.int32
AF = mybir.ActivationFunctionType

D = 2048
NIN = 6272
TT = 256
C = 64
NCH = TT // C
RING = 4
CHW = 2048
FW = 81 * 2048 + 4 * 1024
SMW = 4352
RMS_EPS = 1e-6
GN_EPS = 64e-5
EH = float(np.exp(-0.5))

def vec_layout(Ld):
    off = {}
    n = 0
    for nm, cnt in [("norm_g", Ld * 16), ("mu", Ld * 25), ("w0", Ld * 8), ("a0", Ld * 8), ("k_k", Ld * 8),
                    ("k_a", Ld * 8), ("ln_w", Ld * 8), ("ln_b", Ld * 8), ("r_k", Ld * 8), ("pool_scale", Ld * 8),
                    ("v0", max(Ld - 1, 1) * 8), ("b_pg", Ld * 16), ("final_g", 16)]:
        off[nm] = n
        n += cnt
    return off, n


class KB:
    def __init__(self, nc):
        self.nc = nc
        self.E = {"pe": nc.tensor, "dve": nc.vector, "act": nc.scalar, "pool": nc.gpsimd, "sp": nc.sync}
        self.semh = {}
        for k in self.E:
            self.semh[k] = nc.semaphore("s_" + k).__enter__()
        self.cnt = {k: 0 for k in self.E}
        self.clock = {k: {} for k in self.E}
        self.res = {}
        self.dcnt = {}
        self.nwait = 0
        self.enabled = True
        self.noself = NOSELF
        self._thr_active = False
        self._tl = threading.local()
        self.efree = {}
        self.evt_fin = {}
        self.cnt_model = {}
        self.dry_commit = True

    def _deps(self, r, w):
        d = {}
        for n in r:
            st = self.res.get(n)
            if st and st[0] is not None:
                k, v = st[0]
                if d.get(k, 0) < v:
                    d[k] = v
        for n in w:
            st = self.res.get(n)
            if st:
                if st[0] is not None:
                    k, v = st[0]
                    if d.get(k, 0) < v:
                        d[k] = v
                for k, v in st[1].items():
                    if d.get(k, 0) < v:
                        d[k] = v
        return d

    def _wait(self, eng, d):
        ck = self.clock[eng]
        for k, v in d.items():
            if k == eng and eng in self.noself:
                continue
            if ck.get(k, 0) < v:
                self.E[eng].wait_ge(self.semh[k], v)
                ck[k] = v
                self.nwait += 1

    def _commit(self, ev, r, w):
        k, v = ev
        for n in r:
            st = self.res.setdefault(n, [None, {}])
            if st[1].get(k, 0) < v:
                st[1][k] = v
        for n in w:
            self.res[n] = [ev, {}]

    def run_threads(self, fns):
        fns = [f for f in fns if f is not None]
        n = len(fns)
        if n == 0:
            return
        if n == 1:
            fns[0]()
            return
        assert not self._thr_active
        self._thr_active = True
        self._cond = threading.Condition()
        self._turn = 0
        self._alive = [True] * n
        self._pend = [("start",)] * n
        self._last = 0
        self._exc = None

        def worker(i):
            self._tl.i = i
            with self._cond:
                while self._turn != i:
                    self._cond.wait()
                self._pend[i] = None
            try:
                fns[i]()
            except BaseException as e:
                self._exc = e
            finally:
                with self._cond:
                    self._alive[i] = False
                    self._pend[i] = None
                    self._pick()
                    self._cond.notify_all()
        ths = [threading.Thread(target=worker, args=(i,)) for i in range(n)]
        for t in ths:
            t.start()
        for t in ths:
            t.join()
        self._thr_active = False
        self._tl.i = None
        if self._exc is not None:
            raise self._exc

    def _est_start(self, eng, r, w):
        rdy = 0.0
        for k, v in self._deps(r, w).items():
            f = self.evt_fin.get((k, v), 0.0) + 150.0
            if f > rdy:
                rdy = f
        return max(self.efree.get(eng, 0.0), rdy)

    def _pick(self):
        n = len(self._alive)
        best, bk = None, None
        for d in range(1, n + 1):
            k = (self._last + d) % n
            if not self._alive[k]:
                continue
            p = self._pend[k]
            if p is None:
                continue
            if p[0] == "start":
                st = -2.0
            elif p[0] == "wait":
                if not p[1]():
                    continue
                st = -1.0
            else:
                st = self._est_start(p[1], p[2], p[3])
            if best is None or st < best:
                best, bk = st, k
        if bk is None:
            if any(self._alive):
                self._exc = RuntimeError("scheduler deadlock: all threads blocked in wait_until")
                for k in range(n):
                    if self._alive[k]:
                        bk = k
                        break
            else:
                self._turn = -1
                return
        self._turn = bk
        self._last = bk

    def _sched(self, desc):
        if not self._thr_active:
            return
        i = getattr(self._tl, "i", None)
        if i is None:
            return
        with self._cond:
            self._pend[i] = desc
            self._pick()
            self._cond.notify_all()
            while self._turn != i:
                self._cond.wait()
            self._pend[i] = None
        if self._exc is not None:
            raise self._exc

    def wait_until(self, pred):
        if pred():
            return
        assert self._thr_active, "wait_until outside threads would block forever"
        self._sched(("wait", pred))
        assert pred()

    def tid(self):
        if not self._thr_active:
            return 0
        i = getattr(self._tl, "i", None)
        return 0 if i is None else i

    DEFDUR = {"dve": 500.0, "act": 420.0, "pool": 650.0, "pe": 35.0, "sp": 100.0}

    def _model(self, eng, r, w, dur, ev, async_dur=None):
        st = self._est_start(eng, r, w)
        if async_dur is None:
            fin = st + dur
            self.efree[eng] = fin
        else:
            self.efree[eng] = st + dur
            fin = st + async_dur
        self.evt_fin[ev] = fin

    def op(self, eng, fn, r=(), w=(), dur=None):
        dur = self.DEFDUR[eng] if dur is None else dur
        self._sched(("op", eng, r, w))
        self.cnt_model[eng] = self.cnt_model.get(eng, 0) + 1
        if not self.enabled:
            self._model(eng, r, w, dur, (eng, self.cnt_model[eng]))
            self._commit((eng, self.cnt_model[eng]), r, w)
            return
        assert self.cnt[eng] + 1 == self.cnt_model[eng]
        self._model(eng, r, w, dur, (eng, self.cnt[eng] + 1))
        self._wait(eng, self._deps(r, w))
        inst = fn(self.E[eng])
        self.cnt[eng] += 1
        inst.then_inc(self.semh[eng], 1)
        self._commit((eng, self.cnt[eng]), r, w)

    def group(self, eng, fns, r=(), w=(), dur=None):
        dur = self.DEFDUR[eng] * len(fns) if dur is None else dur
        self._sched(("op", eng, r, w))
        self.cnt_model[eng] = self.cnt_model.get(eng, 0) + 1
        if not self.enabled:
            self._model(eng, r, w, dur, (eng, self.cnt_model[eng]))
            self._commit((eng, self.cnt_model[eng]), r, w)
            return
        assert self.cnt[eng] + 1 == self.cnt_model[eng]
        self._model(eng, r, w, dur, (eng, self.cnt[eng] + 1))
        self._wait(eng, self._deps(r, w))
        inst = None
        for fn in fns:
            inst = fn(self.E[eng])
        self.cnt[eng] += 1
        inst.then_inc(self.semh[eng], 1)
        self._commit((eng, self.cnt[eng]), r, w)

    def dma(self, q, slot, fns, r=(), w=(), sw=True):
        if sw:
            self._sched(("op", q, r, w))
        self.cnt_model[slot] = self.cnt_model.get(slot, 0) + 16 * len(fns)
        if not self.enabled:
            self._model(q, r, w, 100.0, (slot, self.cnt_model[slot]), async_dur=3000.0)
            self._commit((slot, self.cnt_model[slot]), r, w)
            return
        if slot not in self.semh:
            self.semh[slot] = self.nc.semaphore("d_" + slot).__enter__()
            self.dcnt[slot] = 0
        self._model(q, r, w, 100.0, (slot, self.dcnt[slot] + 16 * len(fns)), async_dur=3000.0)
        self._wait(q, self._deps(r, w))
        for fn in fns:
            fn(self.E[q]).then_inc(self.semh[slot], 16)
            self.dcnt[slot] += 16
        self._commit((slot, self.dcnt[slot]), r, w)

    def final_wait(self, eng, names):
        d = {}
        for n in names:
            st = self.res.get(n)
            if st:
                if st[0] is not None:
                    k, v = st[0]
                    d[k] = max(d.get(k, 0), v)
                for k, v in st[1].items():
                    d[k] = max(d.get(k, 0), v)
        self._wait(eng, d)


def wchunk(kind, i):
    if kind == "in":
        return i * 2048, 2048
    if kind == "out":
        return (49 + i) * 2048, 2048
    if kind == "pg":
        return (65 + i) * 2048, 2048
    if kind == "ple":
        return 81 * 2048 + i * 512, 512
    raise ValueError(kind)


def build(T, Ld, chain_f32=False, stop=99, worder=None):
    dry = worder is None
    NT = T // TT
    nc = bass.Bass("TRN2", target_bir_lowering=False)
    voff, NV = vec_layout(Ld)
    CHD = F32 if chain_f32 else BF16

    xT_d = nc.dram_tensor("xT", [D, T], F32, kind="ExternalInput").ap()
    pT_d = nc.dram_tensor("pT", [Ld, 256, T], F32, kind="ExternalInput").ap()
    wst_d = nc.dram_tensor("wst", [Ld, 128, FW], F32, kind="ExternalInput").ap()
    smw_d = nc.dram_tensor("smw", [Ld, 128, SMW], F32, kind="ExternalInput").ap()
    vecs_d = nc.dram_tensor("vecs", [128, NV], F32, kind="ExternalInput").ap()
    out_d = nc.dram_tensor("outT", [D, T], F32, kind="ExternalOutput").ap()
    wsb_d = nc.dram_tensor("wsb", [Ld, 128, FW], BF16, kind="Internal").ap()
    smb_d = nc.dram_tensor("smb", [Ld, 128, SMW], BF16, kind="Internal").ap()

    kb = KB(nc)
    if dry:
        kb.enabled = False

    def sb(name, shape, dt=F32):
        return nc.sbuf_tensor(name, shape, dt).__enter__()

    def ps(name, shape, dt=F32):
        return nc.psum_tensor(name, shape, dt).__enter__()

    xT = sb("xT_s", [128, 16, TT])
    actbf = sb("actbf", [128, 16, TT], BF16)
    yr = sb("yr", [128, 8, TT], BF16)
    ypool = sb("ypool", [128, 8, TT], BF16)
    ring = [sb("ring%d" % i, [128, CHW], BF16) for i in range(RING)]
    smw = sb("smw_s", [128, SMW], BF16)
    vecs = sb("vecs_s", [128, NV])
    nw0 = sb("nw0", [128, Ld * 8])
    cst = sb("cst", [128, 8])
    zraws = [sb("zraw%d" % i, [128, TT + 1]) for i in range(4)]
    vf = sb("vf", [128, 8, TT])
    vb = sb("vb", [128, 8, TT])
    vbf = sb("vbf", [128, 8, TT], BF16)
    AR = sb("AR", [128, 2, 8, TT], BF16)
    KT = sb("KT", [128, 8, TT], BF16)
    BT = sb("BT", [128, 8, TT], BF16)
    Vt = sb("Vt", [128, 8, NCH, 64], BF16)
    Kt = sb("Kt", [128, 8, NCH, 64], BF16)
    Bt = sb("Bt", [128, 8, NCH, 64], BF16)
    sg = sb("sg", [128, 8, TT], BF16)
    ybuf = sb("ybuf", [128, 8, TT])
    ST = sb("ST", [128, Ld, 8, 64])
    STb = sb("STb", [128, Ld, 8, 64], BF16)
    carry = sb("carry", [128, Ld * 25])
    halo = sb("halo", [128, Ld, 8, 16])
    gC = sb("gC", [128, 8, NCH])
    loraIn = sb("loraIn", [128, TT], BF16)
    pbfs = [sb("pbf%d" % i, [128, 2, TT], BF16) for i in range(2)]
    t32 = sb("t32", [32, TT], BF16)
    ident = sb("ident", [128, 128], BF16)
    onesf = sb("onesf", [128, 128])
    bones = sb("bones", [128, 128])
    maskAR = sb("maskAR", [128, 2, 64])
    maskP = sb("maskP", [128, 64])
    identC = sb("identC", [128, 64])
    rmask = sb("rmask", [128, TT])
    invcnt = sb("invcnt", [128, 4, 16])
    iot = sb("iot", [128, 16], I32)
    TN = ["lw", "L", "eL", "eLm", "eLp", "icl", "r", "k", "kk", "sq", "rn", "kkn", "t", "kp", "d", "g1", "g2"]
    tps = []
    for si in range(2):
        dct = {n: sb("t%d_%s" % (si, n), [128, TT]) for n in TN}
        dct["tb"] = dct["lw"]
        dct["rk"] = dct["L"]
        tps.append(dct)
    ALIAS = {"tb": "lw", "rk": "L"}

    def tr(si, n):
        return ("t", si, ALIAS.get(n, n))
    tzs = sb("t_zs", [128, TT])
    trstd = sb("t_rstd", [128, TT])
    pg1 = sb("t_pg1", [128, TT])
    ptt = sb("t_pt", [128, 16])
    ub = [sb("ub%d" % i, [128, 16 + TT]) for i in range(2)]
    us = [sb("us%d" % i, [128, 16 + TT]) for i in range(2)]
    dpool = [sb("dpool%d" % i, [128, TT], BF16) for i in range(2)]
    ABRBs = [sb("ABRB%d" % i, [128, 8, 2, 64], BF16) for i in range(2)]
    AKRKs = [sb("AKRK%d" % i, [128, 8, 2, 64], BF16) for i in range(2)]
    Pb = [sb("Pb%d" % i, [128, 8, 64], CHD) for i in range(2)]
    Qb = [sb("Qb%d" % i, [128, 8, 64], CHD) for i in range(2)]
    Tb = [sb("Tb%d" % i, [128, 8, 64], CHD) for i in range(2)]
    TFs = [sb("TF%d" % i, [128, 8, 64], CHD) for i in range(2)]
    Wsb = sb("Wsb", [128, 8, 64], CHD)
    Usb = sb("Usb", [128, 8, 64], BF16)

    accp = [ps("accp%d" % i, [128, 512]) for i in range(2)]
    miscp = [ps("miscp%d" % i, [128, 512]) for i in range(2)]
    pA = ps("pA", [128, 1024])
    pX = [ps("pX%d" % i, [128, 512]) for i in range(2)]
    accbanks = [(accp[0], "accp0"), (accp[1], "accp1"), (pX[0], "pX0"), (pX[1], "pX1")]
    acc_i = [0]
    misc_i = [0]

    psmap = {"acc": None, "misc": None}

    def next_acc():
        if kb._thr_active:
            bank = psmap["acc"][kb.tid()]
            return bank[0][:, 0:TT], bank[1]
        i = acc_i[0] % 4
        acc_i[0] += 1
        return accbanks[i][0][:, 0:TT], accbanks[i][1]

    def next_misc():
        if kb._thr_active:
            bank = psmap["misc"][kb.tid()]
            return bank[0][:, 0:TT], bank[1]
        i = misc_i[0] % 2
        misc_i[0] += 1
        return miscp[i][:, 0:TT], "miscp%d" % i

    def V(name, idx):
        o = voff[name] + idx
        return vecs[:, o:o + 1]

    wst2 = wst_d.rearrange("l p (a c) -> (l p a) c", c=2048)
    wsb2 = wsb_d.rearrange("l p (a c) -> (l p a) c", c=2048)
    rows = Ld * 128 * (FW // 2048)
    step = 664
    fns = []
    for r0 in range(0, rows, step):
        r1 = min(rows, r0 + step)
        fns.append(lambda e, r0=r0, r1=r1: e.dma_start(out=wsb2[r0:r1, :], in_=wst2[r0:r1, :]))
    smw2 = smw_d.rearrange("l p (a c) -> (l p a) c", c=256)
    smb2 = smb_d.rearrange("l p (a c) -> (l p a) c", c=256)
    rows2 = Ld * 128 * (SMW // 256)
    for r0 in range(0, rows2, 1088):
        r1 = min(rows2, r0 + 1088)
        fns.append(lambda e, r0=r0, r1=r1: e.dma_start(out=smb2[r0:r1, :], in_=smw2[r0:r1, :]))
    kb.dma("pool", "prolog", fns, r=(), w=("wsb", "smb"))
    kb.dma("sp", "vecs", [lambda e: e.dma_start(out=vecs[:], in_=vecs_d[:, :])], w=("vecs",))

    T0 = tps[0]
    kb.op("dve", lambda e: e.memset(onesf[:], 1.0), w=("onesf",))
    kb.op("dve", lambda e: e.memset(bones[:], 0.0), w=("bones",))
    kb.op("dve", lambda e: e.memset(bones[0:64, 0:64], 1.0), w=("bones",))
    kb.op("dve", lambda e: e.memset(bones[64:128, 64:128], 1.0), w=("bones",))
    kb.op("dve", lambda e: e.memset(rmask[:], 1.0), w=("rmask",))
    kb.op("dve", lambda e: e.memset(rmask[:].rearrange("p (c s) -> p c s", s=64)[:, :, 0:1], 0.0), w=("rmask",))
    for i_ in range(2):
        kb.op("dve", lambda e: e.memset(us[i_][:], 0.0), w=(("us", i_),))
    kb.op("dve", lambda e: e.memset(carry[:], 0.0), w=tuple(("carry", i) for i in range(Ld * 25)))
    kb.op("dve", lambda e: e.memset(halo[:], 0.0), w=tuple(("halo", l_, q_) for l_ in range(Ld) for q_ in range(8)))
    kb.op("dve", lambda e: e.memset(ST[:], 0.0), w=tuple(("ST", l_) for l_ in range(Ld)))
    kb.op("dve", lambda e: e.memset(STb[:], 0.0), w=tuple(("STb", l_) for l_ in range(Ld)))
    for i, val in enumerate([0.0, 1.0, -0.5, RMS_EPS, GN_EPS, -1.0, 1e-18]):
        kb.op("dve", lambda e, i=i, val=val: e.memset(cst[:, i:i + 1], val), w=("cst",))
    C0, C1, CM05, CRMS, CGN = (cst[:, i:i + 1] for i in range(5))
    CTINY = cst[:, 6:7]
    kb.op("pool", lambda e: e.affine_select(out=T0["t"][:, 0:128], in_=onesf[:], pattern=[[1, 128]],
                                            compare_op=ALU.is_equal, fill=0.0, base=0, channel_multiplier=-1),
          r=("onesf",), w=(tr(0, "t"),))
    kb.op("dve", lambda e: e.tensor_copy(out=ident[:], in_=T0["t"][:, 0:128]), r=(tr(0, "t"),), w=("ident",))
    for hp in range(2):
        sl = slice(hp * 64, hp * 64 + 64)
        kb.op("pool", lambda e: e.affine_select(out=identC[sl, :], in_=onesf[sl, 0:64], pattern=[[1, 64]],
                                                compare_op=ALU.is_equal, fill=0.0, base=0, channel_multiplier=-1),
              r=("onesf",), w=("identC",))
        kb.op("pool", lambda e: e.affine_select(out=maskAR[sl, 0, :], in_=onesf[sl, 0:64], pattern=[[1, 64]],
                                                compare_op=ALU.is_gt, fill=0.0, base=0, channel_multiplier=-1),
              r=("onesf",), w=("maskAR",))
        kb.op("pool", lambda e: e.affine_select(out=maskAR[sl, 1, :], in_=onesf[sl, 0:64], pattern=[[1, 64]],
                                                compare_op=ALU.is_ge, fill=0.0, base=0, channel_multiplier=-1),
              r=("onesf",), w=("maskAR",))
        kb.op("pool", lambda e: e.affine_select(out=maskP[sl, :], in_=onesf[sl, 0:64], pattern=[[-1, 64]],
                                                compare_op=ALU.is_gt, fill=0.0, base=0, channel_multiplier=1),
              r=("onesf",), w=("maskP",))
    kb.op("pool", lambda e: e.iota(iot[:], pattern=[[1, 16]], base=1, channel_multiplier=0), w=("iot",))
    kb.op("dve", lambda e: e.tensor_copy(out=T0["t"][:, 0:16], in_=iot[:]), r=("iot",), w=(tr(0, "t"),))
    for g in range(4):
        kb.op("dve", lambda e, g=g: e.tensor_scalar(out=T0["d"][:, 0:16], in0=T0["t"][:, 0:16],
                                                    scalar1=float(2 ** (g + 1)), scalar2=None, op0=ALU.min),
              r=(tr(0, "t"),), w=(tr(0, "d"),))
        kb.op("dve", lambda e, g=g: e.reciprocal(out=invcnt[:, g, :], in_=T0["d"][:, 0:16]), r=(tr(0, "d"),),
              w=("invcnt",))
    kb.op("dve", lambda e: e.tensor_scalar(out=nw0[:], in0=vecs[:, voff["w0"]:voff["w0"] + Ld * 8], scalar1=-1.0,
                                           scalar2=None, op0=ALU.mult), r=("vecs",), w=("nw0",))

    order_rec = []
    stream = worder if worder is not None else []
    wpos = [0]
    wissued = [0]
    released = set()

    def issue_w():
        n = wissued[0]
        l, kind, i = stream[n]
        o, sz = wchunk(kind, i)
        slot = n % RING
        kb.dma("sp", "ring%d" % slot,
               [lambda e: e.dma_start(out=ring[slot][:, 0:sz], in_=wsb_d[l, :, o:o + sz])],
               r=("wsb",), w=(("ring", slot),), sw=False)
        wissued[0] += 1

    def pump():
        if dry:
            return
        while wissued[0] < len(stream) and (wissued[0] < RING or (wissued[0] - RING) in released):
            issue_w()

    def next_w(l, kind, i):
        n = wpos[0]
        wpos[0] += 1
        if dry:
            order_rec.append((l, kind, i))
        else:
            assert stream[n] == (l, kind, i), ("weight order mismatch", n, stream[n], (l, kind, i))
            pump()
            assert wissued[0] > n, "weight ring deadlock"
        slot = n % RING
        return ring[slot], ("ring", slot), n

    def release_w(n):
        released.add(n)
        pump()

    def mm_proj(l, kind, i, rhs_of, rhs_res, nk=16):
        wt, wres, wn = next_w(l, kind, i)
        acc, ares = next_acc()
        w3 = wt[:].rearrange("p (k c) -> p k c", c=128)
        kb.group("pe", [lambda e, kc=kc: e.matmul(acc, w3[:, kc, :], rhs_of(kc), start=(kc == 0), stop=(kc == nk - 1))
                        for kc in range(nk)], r=(wres,) + tuple(rhs_res), w=(ares,), dur=137.0 * nk)
        release_w(wn)
        return acc, ares

    def lerp(si, acc, ares, l, cc, out_ap, out_res, zi=None, dn="d"):
        ci = l * 25 + cc
        zi = si if zi is None else zi
        zraw = zraws[zi]
        zr, zr0 = ("zraw", zi), ("zraw0", zi)
        td = tps[si][dn]
        kb.op("act", lambda e: e.activation(out=zraw[:, 1:TT + 1], in_=acc, func=AF.Copy), r=(ares,), w=(zr,))
        kb.op("pool", lambda e: e.tensor_copy(out=zraw[:, 0:1], in_=carry[:, ci:ci + 1]), r=(("carry", ci),),
              w=(zr0,), dur=240.0)
        kb.op("pool", lambda e: e.tensor_tensor(out=td[:], in0=zraw[:, 0:TT], in1=zraw[:, 1:TT + 1],
                                                op=ALU.subtract), r=(zr, zr0), w=(tr(si, dn),))
        kb.op("dve", lambda e: e.scalar_tensor_tensor(out=out_ap, in0=td[:], scalar=V("mu", ci),
                                                      in1=zraw[:, 1:TT + 1], op0=ALU.mult, op1=ALU.add),
              r=(tr(si, dn), zr, "vecs"), w=(out_res,))
        kb.op("pool", lambda e: e.tensor_copy(out=carry[:, ci:ci + 1], in_=zraw[:, TT:TT + 1]), r=(zr,),
              w=(("carry", ci),), dur=240.0)

    def bsum(src_ap, src_res):
        m, mres = next_misc()
        kb.group("pe", [lambda e: e.matmul(m, bones[:], src_ap, start=True, stop=True)],
                 r=("bones",) + tuple(src_res), w=(mres,), dur=250.0)
        return m, mres

    hsl = [slice(0, 64), slice(64, 128)]
    PS2 = {"acc": [(accp[0], "accp0"), (accp[1], "accp1")], "misc": [(miscp[0], "miscp0"), (miscp[1], "miscp1")]}
    pA_lo, pA_hi = pA[:, 0:512], pA[:, 512:1024]
    PS2C = {"acc": [(accp[0], "accp0"), (accp[1], "accp1"), (pX[0], "pX0"), (pX[1], "pX1")],
            "misc": [(miscp[0], "miscp0"), (miscp[1], "miscp1"), (pA_lo, "pA0"), (pA_hi, "pA1")]}
    PS4 = {"acc": [None] * 4, "misc": [(miscp[0], "miscp0"), (miscp[1], "miscp1"), (accp[0], "accp0"), (accp[1], "accp1")]}
    PS_SCAN = {"acc": [None, None, (accp[1], "accp1")], "misc": [None, None, (miscp[1], "miscp1")]}

    def rms_norm(l_unused, write_fn):
        m, mres = next_misc()
        for c in range(16):
            sqn = ["sq", "kk", "rn"][c % 3]
            kb.op("act", lambda e: e.activation(out=T0[sqn][:], in_=xT[:, c, :], func=AF.Square),
                  r=(("xT", c),), w=(tr(0, sqn),))
            kb.group("pe", [lambda e: e.matmul(m, onesf[:], T0[sqn][:], start=(c == 0), stop=(c == 15))],
                     r=("onesf", tr(0, sqn)), w=(mres,), dur=180.0)
        kb.op("act", lambda e: e.activation(out=T0["t"][:], in_=m, func=AF.Ln, bias=CRMS, scale=1.0 / D),
              r=(mres, "cst"), w=(tr(0, "t"),))
        kb.op("act", lambda e: e.activation(out=trstd[:], in_=T0["t"][:], func=AF.Exp, scale=-0.5),
              r=(tr(0, "t"),), w=("t_rstd",))
        for c in range(16):
            write_fn(c)

    for ti in range(NT):
        t0 = ti * TT
        for l in range(Ld):
            if l == 0:
                kb.dma("sp", "xload",
                       [lambda e, c4=c4: e.dma_start(out=xT[:, c4 * 4:c4 * 4 + 4, :],
                                                     in_=xT_d.rearrange("(c p) t -> p c t", p=128)[:, c4 * 4:c4 * 4 + 4, t0:t0 + TT])
                        for c4 in range(4)],
                       w=tuple(("xT", c) for c in range(16)))
            kb.dma("sp", "smw", [lambda e: e.dma_start(out=smw[:], in_=smb_d[l, :, :])], r=("smb",), w=("smw",))
            pbi = (ti * Ld + l) % 2
            pbf = pbfs[pbi]
            pbres = ("pbf", pbi)
            kb.dma("pool", "pload%d" % pbi,
                   [lambda e: e.dma_start(out=pbf[:], in_=pT_d[l].rearrange("(k q) t -> q k t", q=128)[:, :, t0:t0 + TT])],
                   w=(pbres,))
            lora = smw[:, 0:1024]
            vdn = smw[:, 1024:1280].rearrange("p (k c) -> p k c", c=32)
            vup = smw[:, 1280:2304]
            wpool = smw[:, 2304:4352].rearrange("p (g e d) -> p g e d", g=4, e=2)

            rms_norm(l, lambda c: kb.op(
                "dve", lambda e: e.scalar_tensor_tensor(out=actbf[:, c, :], in0=xT[:, c, :],
                                                        scalar=V("norm_g", l * 16 + c), in1=trstd[:],
                                                        op0=ALU.mult, op1=ALU.mult),
                r=(("xT", c), "t_rstd", "vecs"), w=(("actbf", c),)))
            hres = tuple(("actbf", c) for c in range(16))
            hrhs = lambda kc: actbf[:, kc, :]

            acc, ares = mm_proj(l, "in", 24, hrhs, hres)
            lerp(0, acc, ares, l, 24, tzs[:], "t_zs")
            kb.op("act", lambda e: e.activation(out=loraIn[0:64, :], in_=tzs[0:64, :], func=AF.Tanh),
                  r=("t_zs",), w=("loraIn0",))
            kb.op("act", lambda e: e.activation(out=loraIn[64:128, :], in_=tzs[64:128, :], func=AF.Copy),
                  r=("t_zs",), w=("loraIn1",))

            vsrc = vf if l == 0 else vb
            vname = "vf" if l == 0 else "vb"

            def vchain(si):
                for j in range(si, 8, 2):
                    acc, ares = mm_proj(l, "in", 16 + j, hrhs, hres)
                    lerp(si, acc, ares, l, 16 + j, vsrc[:, j, :], (vname, j))
                    kb.op("act", lambda e: e.activation(out=vbf[:, j, :], in_=vsrc[:, j, :], func=AF.Copy),
                          r=((vname, j),), w=(("vbf", j),))
            psmap.update(PS2)
            kb.run_threads([lambda: vchain(0), lambda: vchain(1)])
            if l > 0:
                m, mres = next_misc()
                kb.group("pe", [lambda e, j=j: e.matmul(m[0:32, :], vdn[:, j, :], vbf[:, j, :], start=(j == 0),
                                                        stop=(j == 7)) for j in range(8)],
                         r=("smw",) + tuple(("vbf", j) for j in range(8)), w=(mres,))
                kb.op("act", lambda e: e.activation(out=t32[:], in_=m[0:32, :], func=AF.Copy), r=(mres,), w=("t32",))

                def nuchain(si):
                    tp = tps[si]
                    for j in range(si, 8, 2):
                        m, mres = next_misc()
                        kb.group("pe", [lambda e: e.matmul(m, vup[0:32, j * 128:(j + 1) * 128], t32[:], start=True,
                                                           stop=True)], r=("smw", "t32"), w=(mres,))
                        kb.op("act", lambda e: e.activation(out=tp["g1"][:], in_=m, func=AF.Sigmoid,
                                                            bias=V("v0", (l - 1) * 8 + j), scale=1.0),
                              r=(mres, "vecs"), w=(tr(si, "g1"),))
                        kb.op("pool", lambda e: e.tensor_tensor(out=tp["g2"][:], in0=vf[:, j, :], in1=vb[:, j, :],
                                                               op=ALU.subtract), r=(("vf", j), ("vb", j)),
                              w=(tr(si, "g2"),))
                        kb.op("dve", lambda e: e.tensor_tensor(out=tp["g2"][:], in0=tp["g2"][:], in1=tp["g1"][:],
                                                               op=ALU.mult), r=(tr(si, "g2"), tr(si, "g1")),
                              w=(tr(si, "g2"),))
                        kb.op("pool", lambda e: e.tensor_tensor(out=vb[:, j, :], in0=vb[:, j, :], in1=tp["g2"][:],
                                                               op=ALU.add), r=(("vb", j), tr(si, "g2")),
                              w=(("vb", j),))
                        kb.op("act", lambda e: e.activation(out=vbf[:, j, :], in_=vb[:, j, :], func=AF.Copy),
                              r=(("vb", j),), w=(("vbf", j),))
                psmap.update(PS2)
                kb.run_threads([lambda: nuchain(0), lambda: nuchain(1)])

            pTr = pA[:].bitcast(BF16).rearrange("p (j c k) -> p j c k", j=8, c=NCH)

            def to_tokmajor(src, src_res_fn, dst, dst_res):
                fl = []
                for j in range(8):
                    for c in range(NCH):
                        for hp in range(2):
                            fl.append(lambda e, j=j, c=c, hp=hp: e.transpose(
                                pTr[hsl[hp], j, c, :], src[hsl[hp], j, c * 64:(c + 1) * 64],
                                ident[hsl[hp], hp * 64:hp * 64 + 64]))
                kb.group("pe", fl, r=("ident",) + tuple(src_res_fn(j) for j in range(8)), w=("pA0", "pA1"),
                         dur=4800.0)
                kb.op("act", lambda e: e.activation(out=dst[:], in_=pTr, func=AF.Copy), r=("pA0", "pA1"), w=(dst_res,),
                      dur=2000.0)

            to_tokmajor(vbf, lambda j: ("vbf", j), Vt, "Vt")

            doneX = [False] * 8
            doneY = [False] * 8

            def thrX(si):
                tp = tps[si]
                R_ = lambda n: tr(si, n)
                for j in range(si, 8, 2):
                    cj = l * 8 + j
                    if j >= 2:
                        kb.wait_until(lambda: doneY[j - 2])
                    m, mres = next_misc()
                    kb.group("pe", [lambda e: e.matmul(m, lora[0:64, j * 128:(j + 1) * 128], loraIn[0:64, :],
                                                       start=True, stop=True)], r=("smw", "loraIn0"), w=(mres,), dur=120.0)
                    kb.op("act", lambda e: e.activation(out=tp["lw"][:], in_=m, func=AF.Sigmoid, bias=V("w0", cj),
                                                        scale=1.0), r=(mres, "vecs"), w=(R_("lw"),))
                    kb.op("dve", lambda e: e.tensor_tensor_scan(out=tp["L"][:], data0=rmask[:], data1=tp["lw"][:],
                                                                initial=0.0, op0=ALU.mult, op1=ALU.add),
                          r=("rmask", R_("lw")), w=(R_("L"),))
                    kb.op("act", lambda e: e.activation(out=tp["eL"][:], in_=tp["L"][:], func=AF.Exp, scale=-EH),
                          r=(R_("L"),), w=(R_("eL"),))
                    kb.op("act", lambda e: e.activation(out=tp["eLm"][:], in_=tp["L"][:], func=AF.Exp, scale=EH),
                          r=(R_("L"),), w=(R_("eLm"),))
                    kb.op("pool", lambda e: e.tensor_tensor(out=tp["t"][:], in0=tp["L"][:], in1=tp["lw"][:],
                                                            op=ALU.subtract), r=(R_("L"), R_("lw")), w=(R_("t"),))
                    kb.op("act", lambda e: e.activation(out=tp["eLp"][:], in_=tp["t"][:], func=AF.Exp, scale=-EH),
                          r=(R_("t"),), w=(R_("eLp"),))
                    kb.op("pool", lambda e: e.tensor_copy(
                        out=gC[:, j, :], in_=tp["eL"][:].rearrange("p (c s) -> p c s", s=64)[:, :, 63]),
                        r=(R_("eL"),), w=(("gC", j),), dur=240.0)
                    m, mres = next_misc()
                    kb.group("pe", [lambda e: e.matmul(m, lora[64:128, j * 128:(j + 1) * 128], loraIn[64:128, :],
                                                       start=True, stop=True)], r=("smw", "loraIn1"), w=(mres,), dur=120.0)
                    kb.op("act", lambda e: e.activation(out=tp["icl"][:], in_=m, func=AF.Sigmoid, bias=V("a0", cj),
                                                        scale=1.0), r=(mres, "vecs"), w=(R_("icl"),))
                    acc, ares = mm_proj(l, "in", j, hrhs, hres)
                    lerp(si, acc, ares, l, j, tp["r"][:], R_("r"))
                    kb.op("pool", lambda e: e.tensor_tensor(out=AR[:, 1, j, :], in0=tp["r"][:], in1=tp["eL"][:],
                                                           op=ALU.mult), r=(R_("r"), R_("eL")), w=(("RT", j),))
                    doneX[j] = True

            def thrY(si):
                tp = tps[si]
                R_ = lambda n: tr(si, n)
                for j in range(si, 8, 2):
                    cj = l * 8 + j
                    acc, ares = mm_proj(l, "in", 8 + j, hrhs, hres)
                    lerp(si, acc, ares, l, 8 + j, tp["k"][:], R_("k"), zi=2 + si, dn="sq")
                    kb.op("dve", lambda e: e.tensor_scalar(out=tp["kk"][:], in0=tp["k"][:], scalar1=V("k_k", cj),
                                                           scalar2=None, op0=ALU.mult), r=(R_("k"), "vecs"),
                          w=(R_("kk"),))
                    kb.op("act", lambda e: e.activation(out=tp["sq"][:], in_=tp["kk"][:], func=AF.Square),
                          r=(R_("kk"),), w=(R_("sq"),))
                    m, mres = bsum(tp["sq"][:], (R_("sq"),))
                    kb.op("act", lambda e: e.activation(out=tp["rn"][:], in_=m, func=AF.Ln, bias=CTINY, scale=1.0),
                          r=(mres, "cst"), w=(R_("rn"),))
                    kb.op("act", lambda e: e.activation(out=tp["rn"][:], in_=tp["rn"][:], func=AF.Exp, scale=-0.5),
                          r=(R_("rn"),), w=(R_("rn"),))
                    kb.op("pool", lambda e: e.tensor_tensor(out=tp["kkn"][:], in0=tp["kk"][:], in1=tp["rn"][:],
                                                           op=ALU.mult), r=(R_("kk"), R_("rn")), w=(R_("kkn"),))
                    kb.wait_until(lambda: doneX[j])
                    kb.op("dve", lambda e: e.tensor_scalar(out=tp["g1"][:], in0=tp["icl"][:], scalar1=-1.0,
                                                           scalar2=V("k_a", cj), op0=ALU.add, op1=ALU.mult),
                          r=(R_("icl"), "vecs"), w=(R_("g1"),))
                    kb.op("dve", lambda e: e.scalar_tensor_tensor(out=tp["kp"][:], in0=tp["g1"][:], scalar=1.0,
                                                                  in1=tp["k"][:], op0=ALU.add, op1=ALU.mult),
                          r=(R_("g1"), R_("k")), w=(R_("kp"),))
                    kb.op("dve", lambda e: e.tensor_tensor(out=KT[:, j, :], in0=tp["kp"][:], in1=tp["eLm"][:],
                                                           op=ALU.mult), r=(R_("kp"), R_("eLm")), w=(("KT", j),))
                    kb.op("dve", lambda e: e.scalar_tensor_tensor(out=AR[:, 0, j, :], in0=tp["kkn"][:], scalar=-1.0,
                                                                  in1=tp["eLp"][:], op0=ALU.mult, op1=ALU.mult),
                          r=(R_("kkn"), R_("eLp")), w=(("AT", j),))
                    kb.op("pool", lambda e: e.tensor_tensor(out=tp["g2"][:], in0=tp["kkn"][:], in1=tp["icl"][:],
                                                           op=ALU.mult), r=(R_("kkn"), R_("icl")), w=(R_("g2"),))
                    kb.op("pool", lambda e: e.tensor_tensor(out=BT[:, j, :], in0=tp["g2"][:], in1=tp["eLm"][:],
                                                           op=ALU.mult), r=(R_("g2"), R_("eLm")), w=(("BT", j),))
                    kb.op("dve", lambda e: e.scalar_tensor_tensor(out=tp["kk"][:], in0=tp["r"][:],
                                                                  scalar=V("r_k", cj), in1=tp["kp"][:],
                                                                  op0=ALU.mult, op1=ALU.mult),
                          r=(R_("r"), R_("kp"), "vecs"), w=(R_("kk"),))
                    m, mres = bsum(tp["kk"][:], (R_("kk"),))
                    kb.op("dve", lambda e: e.tensor_tensor(out=vb[:, j, :], in0=m, in1=vsrc[:, j, :], op=ALU.mult),
                          r=(mres, (vname, j)), w=(("vb", j),))

                    doneY[j] = True

            def gates_pool():
                for j in range(8):
                    acc, ares = mm_proj(l, "in", 25 + j, hrhs, hres)
                    kb.op("act", lambda e: e.activation(out=sg[:, j, :], in_=acc, func=AF.Silu), r=(ares,),
                          w=(("sg", j),))
                for g in range(4):
                    win = 2 ** (g + 1)
                    for e_ in range(2):
                        q = 2 * g + e_
                        acc, ares = mm_proj(l, "in", 33 + q, hrhs, hres)
                        u = ub[e_]
                        kb.op("act", lambda e: e.activation(out=u[:, 16:16 + TT], in_=acc, func=AF.Copy), r=(ares,),
                              w=(("ub", e_),))
                        kb.op("pool", lambda e: e.tensor_copy(out=u[:, 0:16], in_=halo[:, l, q, :]),
                              r=(("halo", l, q),), w=(("ubh", e_),), dur=240.0)
                        cur, curres = u, (("ub", e_), ("ubh", e_))
                        stp = 1
                        k_ = 0
                        while stp < win:
                            dst = us[k_ % 2]
                            dres = ("us", k_ % 2)
                            kb.op("pool", lambda e: e.tensor_tensor(out=dst[:, stp:16 + TT], in0=cur[:, stp:16 + TT],
                                                                    in1=cur[:, 0:16 + TT - stp], op=ALU.add),
                                  r=curres, w=(dres,))
                            cur, curres = dst, (dres,)
                            stp *= 2
                            k_ += 1
                        kb.op("pool", lambda e: e.tensor_copy(out=halo[:, l, q, :], in_=u[:, TT:TT + 16]),
                              r=(("ub", e_), ("ubh", e_)), w=(("halo", l, q),), dur=240.0)
                        kb.op("dve", lambda e: e.scalar_tensor_tensor(out=dpool[e_][:], in0=cur[:, 16:16 + TT],
                                                                      scalar=1.0 / win, in1=u[:, 16:16 + TT],
                                                                      op0=ALU.mult, op1=ALU.subtract),
                              r=curres + (("ub", e_),), w=(("dpool", e_),))
                        if ti == 0:
                            kb.op("dve", lambda e: e.tensor_tensor(out=ptt[:], in0=cur[:, 16:32],
                                                                   in1=invcnt[:, g, :], op=ALU.mult),
                                  r=curres + ("invcnt",), w=("t_pt",))
                            kb.op("dve", lambda e: e.tensor_tensor(out=dpool[e_][:, 0:16], in0=ptt[:],
                                                                   in1=u[:, 16:32], op=ALU.subtract),
                                  r=("t_pt", ("ub", e_)), w=(("dpool", e_),))
                    for od in range(2):
                        q = 2 * g + od
                        m, mres = next_misc()
                        kb.group("pe", [lambda e, e_=e_: e.matmul(m, wpool[:, g, e_, od * 128:(od + 1) * 128],
                                                                  dpool[e_][:], start=(e_ == 0), stop=(e_ == 1))
                                        for e_ in range(2)], r=("smw", ("dpool", 0), ("dpool", 1)), w=(mres,), dur=280.0)
                        acc, ares = mm_proj(l, "in", 41 + q, hrhs, hres)
                        kb.op("act", lambda e: e.activation(out=pg1[:], in_=acc, func=AF.Silu), r=(ares,),
                              w=("t_pg1",))
                        kb.op("dve", lambda e: e.scalar_tensor_tensor(out=ypool[:, q, :], in0=m,
                                                                      scalar=V("pool_scale", l * 8 + q),
                                                                      in1=pg1[:], op0=ALU.mult, op1=ALU.mult),
                              r=(mres, "t_pg1", "vecs"), w=(("ypool", q),))

            psmap.update(PS2C)
            kb.run_threads([lambda: thrX(0), lambda: thrX(1), lambda: thrY(0), lambda: thrY(1)])
            to_tokmajor(KT, lambda j: ("KT", j), Kt, "Kt")
            to_tokmajor(BT, lambda j: ("BT", j), Bt, "Bt")

            STl = ST[:, l, :, :]
            STbl = STb[:, l, :, :]
            allAT = tuple(("AT", j) for j in range(8))
            allRT = tuple(("RT", j) for j in range(8))
            allKT = tuple(("KT", j) for j in range(8))
            allBT = tuple(("BT", j) for j in range(8))
            pA4 = pA[:].rearrange("p (j a t) -> p j a t", j=8, a=2)
            pX3 = [pX[i][:].rearrange("p (j t) -> p j t", j=8) for i in range(2)]
            pW3 = accp[0][:].rearrange("p (j t) -> p j t", j=8)
            pU3 = accp[0][:].rearrange("p (j t) -> p j t", j=8)
            pY3 = miscp[0][:].rearrange("p (j t) -> p j t", j=8)
            pS3 = miscp[0][:].rearrange("p (j t) -> p j t", j=8)
            mAR = maskAR[:].unsqueeze(1).broadcast_to([128, 8, 2, 64])
            mP = maskP[:].unsqueeze(1).broadcast_to([128, 8, 64])
            iC = identC[:].unsqueeze(1).broadcast_to([128, 8, 64])
            JH = [(j, hp) for j in range(8) for hp in range(2)]

            def stageA(c):
                tok = slice(c * 64, (c + 1) * 64)
                ABRB, AKRK, TF = ABRBs[c % 2], AKRKs[c % 2], TFs[c % 2]
                rAB, rAK, rTF = ("ABRB", c % 2), ("AKRK", c % 2), ("TF", c % 2)
                kb.group("pe", [lambda e, j=j, hp=hp: e.matmul(pA4[hsl[hp], j, :, :], BT[hsl[hp], j, tok],
                                                               AR[hsl[hp], :, j, tok], start=True, stop=True)
                                for j, hp in JH], r=allBT + allAT + allRT, w=("pA0", "pA1"))
                kb.op("dve", lambda e: e.tensor_tensor(out=ABRB[:], in0=pA4, in1=mAR, op=ALU.mult),
                      r=("pA0", "pA1", "maskAR"), w=(rAB,), dur=1100.0)
                kb.group("pe", [lambda e, j=j, hp=hp: e.matmul(pX3[0][hsl[hp], j, :], AR[hsl[hp], 0, j, tok],
                                                               BT[hsl[hp], j, tok], start=True, stop=True)
                                for j, hp in JH], r=allBT + allAT, w=("pX0",))
                kb.op("dve", lambda e: e.tensor_tensor(out=Pb[0][:], in0=pX3[0], in1=mP, op=ALU.mult),
                      r=("pX0", "maskP"), w=("Pb0",), dur=650.0)
                kb.group("pe", [lambda e, j=j, hp=hp: e.matmul(pA4[hsl[hp], j, :, :], KT[hsl[hp], j, tok],
                                                               AR[hsl[hp], :, j, tok], start=True, stop=True)
                                for j, hp in JH], r=allKT + allAT + allRT, w=("pA0", "pA1"))
                kb.op("act", lambda e: e.activation(out=Qb[0][:], in_=ABRB[:, :, 0, :], func=AF.Copy),
                      r=(rAB,), w=("Qb0",), dur=650.0)
                kb.op("dve", lambda e: e.tensor_tensor(out=AKRK[:], in0=pA4, in1=mAR, op=ALU.mult),
                      r=("pA0", "pA1", "maskAR"), w=(rAK,), dur=1100.0)
                kb.op("dve", lambda e: e.tensor_tensor(out=Tb[0][:], in0=ABRB[:, :, 0, :], in1=iC, op=ALU.add),
                      r=(rAB, "identC"), w=("Tb0",), dur=650.0)
                for lev in range(5):
                    a, b = lev % 2, (lev + 1) % 2
                    kb.group("pe", [lambda e, j=j, hp=hp: e.matmul(pX3[0][hsl[hp], j, :], Qb[a][hsl[hp], j, :],
                                                                   Pb[a][hsl[hp], j, :], start=True, stop=True)
                                    for j, hp in JH], r=("Qb%d" % a, "Pb%d" % a), w=("pX0",))
                    kb.op("dve", lambda e: e.tensor_copy(out=Pb[b][:], in_=pX3[0]), r=("pX0",), w=("Pb%d" % b,), dur=650.0)
                    if lev < 4:
                        kb.group("pe", [lambda e, j=j, hp=hp: e.matmul(pX3[1][hsl[hp], j, :], Pb[a][hsl[hp], j, :],
                                                                       Qb[a][hsl[hp], j, :], start=True, stop=True)
                                        for j, hp in JH], r=("Qb%d" % a, "Pb%d" % a), w=("pX1",))
                        kb.op("act", lambda e: e.activation(out=Qb[b][:], in_=pX3[1], func=AF.Copy),
                              r=("pX1",), w=("Qb%d" % b,), dur=650.0)
                    kb.group("pe", [lambda e, j=j, hp=hp: e.matmul(pA4[hsl[hp], j, 0, :], Pb[b][hsl[hp], j, :],
                                                                   Tb[a][hsl[hp], j, :], start=True, stop=True)
                                    for j, hp in JH], r=("Pb%d" % b, "Tb%d" % a), w=("pA0", "pA1"))
                    tdst, tdres = (Tb[b], "Tb%d" % b) if lev < 4 else (TF, rTF)
                    kb.op("dve", lambda e: e.tensor_tensor(out=tdst[:], in0=pA4[:, :, 0, :], in1=Tb[a][:],
                                                           op=ALU.add), r=("pA0", "pA1", "Tb%d" % a), w=(tdres,), dur=650.0)

            def stageB(c):
                tok = slice(c * 64, (c + 1) * 64)
                ABRB, AKRK, TF = ABRBs[c % 2], AKRKs[c % 2], TFs[c % 2]
                rAB, rAK, rTF = ("ABRB", c % 2), ("AKRK", c % 2), ("TF", c % 2)
                fl = []
                for j, hp in JH:
                    fl.append(lambda e, j=j, hp=hp: e.matmul(pW3[hsl[hp], j, :], AR[hsl[hp], 0, j, tok],
                                                             STbl[hsl[hp], j, :], start=True, stop=False))
                    fl.append(lambda e, j=j, hp=hp: e.matmul(pW3[hsl[hp], j, :], AKRK[hsl[hp], j, 0, :],
                                                             Vt[hsl[hp], j, c, :], start=False, stop=True))
                kb.group("pe", fl, dur=33.0 * len(fl), r=allAT + (("STb", l), rAK, "Vt"), w=("accp0",))
                kb.op("act", lambda e: e.activation(out=Wsb[:], in_=pW3, func=AF.Copy), r=("accp0",), w=("Wsb",), dur=650.0)
                kb.group("pe", [lambda e, j=j, hp=hp: e.matmul(pU3[hsl[hp], j, :], TF[hsl[hp], j, :],
                                                               Wsb[hsl[hp], j, :], start=True, stop=True)
                                for j, hp in JH], r=(rTF, "Wsb"), w=("accp0",))
                kb.op("act", lambda e: e.activation(out=Usb[:], in_=pU3, func=AF.Copy), r=("accp0",), w=("Usb",), dur=650.0)
                fl = []
                for j, hp in JH:
                    o = pY3[hsl[hp], j, :]
                    fl.append(lambda e, j=j, hp=hp, o=o: e.matmul(o, STbl[hsl[hp], j, :], AR[hsl[hp], 1, j, tok],
                                                                  start=True, stop=False))
                    fl.append(lambda e, j=j, hp=hp, o=o: e.matmul(o, Usb[hsl[hp], j, :], ABRB[hsl[hp], j, 1, :],
                                                                  start=False, stop=False))
                    fl.append(lambda e, j=j, hp=hp, o=o: e.matmul(o, Vt[hsl[hp], j, c, :], AKRK[hsl[hp], j, 1, :],
                                                                  start=False, stop=True))
                kb.group("pe", fl, dur=33.0 * len(fl), r=allRT + (("STb", l), "Usb", rAB, rAK, "Vt"), w=("miscp0",))
                kb.op("act", lambda e: e.activation(out=ybuf[:, :, tok], in_=pY3, func=AF.Copy), r=("miscp0",),
                      w=(("ybuf", c),))
                fl = []
                for j, hp in JH:
                    o = pS3[hsl[hp], j, :]
                    fl.append(lambda e, j=j, hp=hp, o=o: e.matmul(o, Bt[hsl[hp], j, c, :], Usb[hsl[hp], j, :],
                                                                  start=True, stop=False))
                    fl.append(lambda e, j=j, hp=hp, o=o: e.matmul(o, Kt[hsl[hp], j, c, :], Vt[hsl[hp], j, c, :],
                                                                  start=False, stop=True))
                kb.group("pe", fl, dur=33.0 * len(fl), r=("Bt", "Kt", "Vt", "Usb"), w=("miscp0",))
                kb.op("dve", lambda e: e.tensor_tensor(out=STl, in0=pS3, in1=STl, op=ALU.add),
                      r=("miscp0", ("ST", l)), w=(("ST", l),))
                gcb = gC[:, :, c:c + 1].broadcast_to([128, 8, 64])
                kb.op("dve", lambda e: e.tensor_tensor(out=STl, in0=STl, in1=gcb, op=ALU.mult),
                      r=(("ST", l),) + tuple(("gC", j) for j in range(8)), w=(("ST", l),))
                kb.op("act", lambda e: e.activation(out=STbl, in_=STl, func=AF.Copy), r=(("ST", l),),
                      w=(("STb", l),))

            doneA = [False] * NCH
            doneB = [False] * NCH

            def thrA():
                for c in range(NCH):
                    if c >= 2:
                        kb.wait_until(lambda: doneB[c - 2])
                    stageA(c)
                    doneA[c] = True

            def thrB():
                for c in range(NCH):
                    kb.wait_until(lambda: doneA[c])
                    stageB(c)
                    doneB[c] = True
            psmap.update(PS_SCAN)
            kb.run_threads([thrA, thrB, gates_pool])

            yres = tuple(("ybuf", c) for c in range(NCH))

            def gnchain(k4):
                si = k4 % 2
                tp = tps[si]
                ng2, nsq, nt = ("g2", "sq", "t") if k4 < 2 else ("g1", "kk", "rn")
                R_ = lambda n: tr(si, n)
                for j in range(k4, 8, 4):
                    cj = l * 8 + j
                    m, mres = next_misc()
                    kb.group("pe", [lambda e: e.matmul(m, bones[:], ybuf[:, j, :], start=True, stop=True)],
                             r=("bones",) + yres, w=(mres,), dur=250.0)
                    kb.op("dve", lambda e: e.scalar_tensor_tensor(out=tp[ng2][:], in0=m, scalar=-1.0 / 64,
                                                                  in1=ybuf[:, j, :], op0=ALU.mult, op1=ALU.add),
                          r=(mres,) + yres, w=(R_(ng2),))
                    kb.op("act", lambda e: e.activation(out=tp[nsq][:], in_=tp[ng2][:], func=AF.Square),
                          r=(R_(ng2),), w=(R_(nsq),))
                    m, mres = bsum(tp[nsq][:], (R_(nsq),))
                    kb.op("act", lambda e: e.activation(out=tp[nt][:], in_=m, func=AF.Ln, bias=CGN,
                                                        scale=1.0 / 64), r=(mres, "cst"), w=(R_(nt),))
                    kb.op("act", lambda e: e.activation(out=tp[nt][:], in_=tp[nt][:], func=AF.Exp, scale=-0.5),
                          r=(R_(nt),), w=(R_(nt),))
                    kb.op("dve", lambda e: e.tensor_tensor(out=tp[ng2][:], in0=tp[ng2][:], in1=tp[nt][:],
                                                           op=ALU.mult), r=(R_(ng2), R_(nt)), w=(R_(ng2),))
                    kb.op("dve", lambda e: e.tensor_scalar(out=tp[ng2][:], in0=tp[ng2][:], scalar1=V("ln_w", cj),
                                                           scalar2=V("ln_b", cj), op0=ALU.mult, op1=ALU.add),
                          r=(R_(ng2), "vecs"), w=(R_(ng2),))
                    kb.op("pool", lambda e: e.tensor_tensor(out=tp[ng2][:], in0=tp[ng2][:], in1=vb[:, j, :],
                                                            op=ALU.add), r=(R_(ng2), ("vb", j)), w=(R_(ng2),))
                    kb.op("pool", lambda e: e.tensor_tensor(out=yr[:, j, :], in0=tp[ng2][:], in1=sg[:, j, :],
                                                            op=ALU.mult), r=(R_(ng2), ("sg", j)), w=(("yr", j),))
            psmap.update(PS4)
            kb.run_threads([lambda: gnchain(0), lambda: gnchain(1), lambda: gnchain(2), lambda: gnchain(3)])

            ymres = tuple(("yr", j) for j in range(8)) + tuple(("ypool", q) for q in range(8))
            yrhs = lambda kc: (yr[:, kc, :] if kc < 8 else ypool[:, kc - 8, :])
            for d in range(16):
                acc, ares = mm_proj(l, "out", d, yrhs, ymres)
                kb.op("dve", lambda e: e.tensor_tensor(out=xT[:, d, :], in0=acc, in1=xT[:, d, :], op=ALU.add),
                      r=(ares, ("xT", d)), w=(("xT", d),))
                kb.op("act", lambda e: e.activation(out=actbf[:, d, :], in_=xT[:, d, :], func=AF.Copy),
                      r=(("xT", d),), w=(("actbf", d),))

            for pe_ in range(8):
                wple_t, wple_res, wple_n = next_w(l, "ple", pe_)
                wple = wple_t[:, 0:512].rearrange("p (a k c) -> p a k c", a=2, k=2)
                pm = []
                for dd in range(2):
                    m, mres = next_misc()
                    kb.group("pe", [lambda e, k=k: e.matmul(m, wple[:, dd, k, :], pbf[:, k, :], start=(k == 0),
                                                            stop=(k == 1)) for k in range(2)],
                             r=(wple_res, pbres), w=(mres,), dur=230.0)
                    pm.append((m, mres))
                release_w(wple_n)
                for dd in range(2):
                    d = pe_ * 2 + dd
                    tp = tps[dd]
                    m, mres = pm[dd]
                    acc, ares = mm_proj(l, "pg", d, hrhs, hres)
                    kb.op("act", lambda e: e.activation(out=tp["g1"][:], in_=acc, func=AF.Sigmoid,
                                                        bias=V("b_pg", l * 16 + d), scale=1.0),
                          r=(ares, "vecs"), w=(tr(dd, "g1"),))
                    kb.op("dve", lambda e: e.tensor_tensor(out=tp["g2"][:], in0=m, in1=tp["g1"][:], op=ALU.mult),
                          r=(mres, tr(dd, "g1")), w=(tr(dd, "g2"),))
                    kb.op("pool", lambda e: e.tensor_tensor(out=xT[:, d, :], in0=xT[:, d, :], in1=tp["g2"][:],
                                                            op=ALU.add), r=(("xT", d), tr(dd, "g2")), w=(("xT", d),))

            if l == Ld - 1:
                rms_norm(l, lambda c: kb.op(
                    "dve", lambda e: e.scalar_tensor_tensor(out=xT[:, c, :], in0=xT[:, c, :],
                                                            scalar=V("final_g", c), in1=trstd[:],
                                                            op0=ALU.mult, op1=ALU.mult),
                    r=(("xT", c), "t_rstd", "vecs"), w=(("xT", c),)))
                kb.dma("sp", "xstore",
                       [lambda e, c4=c4: e.dma_start(out=out_d.rearrange("(c p) t -> p c t", p=128)[:, c4 * 4:c4 * 4 + 4, t0:t0 + TT],
                                                     in_=xT[:, c4 * 4:c4 * 4 + 4, :])
                        for c4 in range(4)],
                       r=tuple(("xT", c) for c in range(16)), w=("out",))

    if dry:
        return order_rec
    kb.final_wait("sp", ["out"])
    return nc, kb


def build2(T, Ld, **kw):
    order = build(T, Ld, worder=None, **kw)
    return build(T, Ld, worder=order, **kw)


def _prep_common(inputs, Ld):
    f = lambda a: np.asarray(a, dtype=np.float32)
    w_in, w_out, w_pg, w_ple = f(inputs["w_in"]), f(inputs["w_out"]), f(inputs["w_pg"]), f(inputs["w_ple"])
    wst = np.empty((Ld, 128, FW), np.float32)
    for l in range(Ld):
        parts = []
        wi = w_in[l].reshape(16, 128, NIN)
        for cb in range(49):
            parts.append(wi[:, :, cb * 128:(cb + 1) * 128].transpose(1, 0, 2).reshape(128, 2048))
        wo = w_out[l].reshape(16, 128, D)
        for d in range(16):
            parts.append(wo[:, :, d * 128:(d + 1) * 128].transpose(1, 0, 2).reshape(128, 2048))
        wg = w_pg[l].reshape(16, 128, D)
        for d in range(16):
            parts.append(wg[:, :, d * 128:(d + 1) * 128].transpose(1, 0, 2).reshape(128, 2048))
        wp = w_ple[l].reshape(2, 128, 16, 128)
        for e in range(8):
            parts.append(wp[:, :, e * 2:(e + 1) * 2, :].transpose(1, 2, 0, 3).reshape(128, 512))
        wst[l] = np.concatenate(parts, axis=1)
    smw = np.zeros((Ld, 128, SMW), np.float32)
    w_up, a_up, v_down, v_up, w_pool = (f(inputs[k]) for k in ["w_up", "a_up", "v_down", "v_up", "w_pool"])
    for l in range(Ld):
        smw[l, 0:64, 0:1024] = w_up[l]
        smw[l, 64:128, 0:1024] = a_up[l]
        if l > 0:
            smw[l, :, 1024:1280] = v_down[l - 1].reshape(8, 128, 32).transpose(1, 0, 2).reshape(128, 256)
            smw[l, 0:32, 1280:2304] = v_up[l - 1]
        smw[l, :, 2304:4352] = w_pool[l].reshape(4, 2, 128, 256).transpose(2, 0, 1, 3).reshape(128, 2048)
    voff, NV = vec_layout(Ld)
    vecs = np.zeros((128, NV), np.float32)

    def put(name, arr, per):
        a = f(arr).reshape(-1, per, 128).transpose(2, 0, 1).reshape(128, -1)
        vecs[:, voff[name]:voff[name] + a.shape[1]] = a
    put("norm_g", inputs["norm_g"][:Ld], 16)
    put("mu", inputs["mu"][:Ld], 25)
    for nm in ["w0", "a0", "k_k", "k_a", "ln_w", "ln_b", "pool_scale"]:
        put(nm, inputs[nm][:Ld], 8)
    put("r_k", f(inputs["r_k"])[:Ld].reshape(Ld, 1024), 8)
    if Ld > 1:
        put("v0", inputs["v0"][:Ld - 1], 8)
    put("b_pg", inputs["b_pg"][:Ld], 16)
    put("final_g", f(inputs["final_g"]).reshape(1, D), 16)
    return wst, smw, vecs


NOSELF = ("pe",)
_CACHE = {}
STOP = 99


def kernel(**inputs):
    x = np.asarray(inputs["x"], dtype=np.float32)
    p = np.asarray(inputs["p"], dtype=np.float32)
    B, T, _ = x.shape
    Ld = p.shape[0]
    assert B == 8 and T % TT == 0
    wst, smw, vecs = _prep_common(inputs, Ld)
    key = (T, Ld)
    if key not in _CACHE:
        _CACHE[key] = build2(T, Ld)[0]
    nc = _CACHE[key]
    in_maps = []
    for b in range(B):
        in_maps.append({
            "xT": np.ascontiguousarray(x[b].T),
            "pT": np.ascontiguousarray(p[:, b].transpose(0, 2, 1)),
            "wst": wst, "smw": smw, "vecs": vecs,
        })
    res = run_bass_kernel_spmd(nc, in_maps, core_ids=list(range(B)))
    out = np.stack([np.asarray(r["outT"]).T for r in res.results], axis=0)
    return np.ascontiguousarray(out.astype(np.float32))
```
